# Optimizing a Trainium2 kernel written in Bass

```python
import jax, jax.numpy as jnp
from jax import lax
import numpy as np

D_MODEL = 1024
BATCH = 8
SEQ = 4096
DEPTH = 1

CHUNK = 128
SGU_GROUPS = 8
SGU_GROUP_DIM = D_MODEL // SGU_GROUPS
SGU_WIDTH = SGU_GROUPS * SGU_GROUP_DIM
ATTN_HEADS = 8
ATTN_HEAD_DIM = 128
ATTN_WIDTH = ATTN_HEADS * ATTN_HEAD_DIM
IDX_HEADS = 8
IDX_DIM = 64
IDX_TOPK_MAX = 256
QUERY_BLOCK = 64
FFN_DIM = 4 * D_MODEL
ALPHA = (2.0 * DEPTH) ** 0.25
BETA = (8.0 * DEPTH) ** -0.25
LN_EPS = 1e-5
IN_SPLITS = (SGU_WIDTH, SGU_WIDTH, ATTN_WIDTH, ATTN_WIDTH, ATTN_WIDTH,
             IDX_HEADS * IDX_DIM, IDX_DIM, IDX_HEADS, D_MODEL, D_MODEL)
IN_WIDTH = sum(IN_SPLITS)

kernel_name = "hybrid_gmlp_dsa_gated_deepnorm"


def layer_norm(x, g, b):
    xf = x.astype(jnp.float32)
    mu = jnp.mean(xf, axis=-1, keepdims=True)
    xc = xf - mu
    var = jnp.mean(xc * xc, axis=-1, keepdims=True)
    return (xc * lax.rsqrt(var + LN_EPS) * g.astype(jnp.float32) + b.astype(jnp.float32)).astype(x.dtype)


def sgu_mixer(u, v, ln_g, ln_b, w_s, b_s):
    bsz, seq, _ = v.shape
    u = jax.nn.gelu(u)
    v = layer_norm(jax.nn.gelu(v), ln_g, ln_b)
    v = v.reshape(bsz, seq // CHUNK, CHUNK, SGU_GROUPS, SGU_GROUP_DIM)
    causal = jnp.tril(jnp.ones((CHUNK, CHUNK), dtype=bool))
    w = jnp.where(causal[None], w_s, jnp.zeros_like(w_s))
    s = jnp.einsum('gts,bnsgc->bntgc', w, v) + b_s.T[None, None, :, :, None]
    return u * s.reshape(bsz, seq, SGU_WIDTH)


def dsa_mixer(q, k, v, q_idx, k_idx, w_idx):
    bsz, seq, _ = q.shape
    top_k = min(IDX_TOPK_MAX, seq // 4)
    q = q.reshape(bsz, seq, ATTN_HEADS, ATTN_HEAD_DIM)
    k = k.reshape(bsz, seq, ATTN_HEADS, ATTN_HEAD_DIM)
    v = v.reshape(bsz, seq, ATTN_HEADS, ATTN_HEAD_DIM)
    q_idx = q_idx.reshape(bsz, seq, IDX_HEADS, IDX_DIM)
    w_idx = w_idx * (IDX_HEADS ** -0.5 * IDX_DIM ** -0.5)
    key_pos = jnp.arange(seq)
    gather = jax.vmap(lambda arr, ids: arr[ids])

    def block(start):
        qb = lax.dynamic_slice_in_dim(q, start, QUERY_BLOCK, axis=1)
        qib = lax.dynamic_slice_in_dim(q_idx, start, QUERY_BLOCK, axis=1)
        wb = lax.dynamic_slice_in_dim(w_idx, start, QUERY_BLOCK, axis=1)
        q_pos = start + jnp.arange(QUERY_BLOCK)
        causal = key_pos[None, :] <= q_pos[:, None]
        logits = jnp.einsum('bthd,bsd->bths', qib, k_idx)
        score = jnp.einsum('bth,bths->bts', wb, jax.nn.relu(logits)).astype(jnp.float32)
        score = jnp.where(causal[None], score, -jnp.inf)
        _, idx = lax.top_k(score, top_k)
        valid = idx <= q_pos[None, :, None]
        kg = gather(k, idx)
        vg = gather(v, idx)
        att = jnp.einsum('bthd,btkhd->bthk', qb, kg).astype(jnp.float32) * (ATTN_HEAD_DIM ** -0.5)
        att = jnp.where(valid[:, :, None, :], att, -jnp.inf)
        p = jax.nn.softmax(att, axis=-1).astype(v.dtype)
        return jnp.einsum('bthk,btkhd->bthd', p, vg)

    starts = jnp.arange(0, seq, QUERY_BLOCK)
    out = lax.map(block, starts)
    return out.transpose(1, 0, 2, 3, 4).reshape(bsz, seq, ATTN_WIDTH)


def hybrid_layer(x, w_in, sgu_ln_g, sgu_ln_b, sgu_w, sgu_b, w_branch_a, w_branch_b, w_out,
                 ln1_g, ln1_b, w_ffn_up, w_ffn_down, ln2_g, ln2_b):
    proj = x @ w_in
    offsets = [int(o) for o in np.cumsum(IN_SPLITS)[:-1]]
    u_a, v_a, q, k, v, q_idx, k_idx, w_idx, g_a, g_b = jnp.split(proj, offsets, axis=-1)
    a = sgu_mixer(u_a, v_a, sgu_ln_g, sgu_ln_b, sgu_w, sgu_b)
    b = dsa_mixer(q, k, v, q_idx, k_idx, w_idx)
    merged = jax.nn.sigmoid(g_a) * (a @ w_branch_a) + jax.nn.sigmoid(g_b) * (b @ w_branch_b)
    x = layer_norm(ALPHA * x + merged @ w_out, ln1_g, ln1_b)
    h = jnp.square(jax.nn.relu(x @ w_ffn_up))
    return layer_norm(ALPHA * x + h @ w_ffn_down, ln2_g, ln2_b)


def setup_inputs(seed: int = 0) -> dict:
    key = jax.random.key(seed)
    ks = jax.random.split(key, 16)
    n = lambda k, shape: jax.random.normal(k, shape, dtype=jnp.float32)
    L = DEPTH
    return {
        "x": n(ks[0], (BATCH, SEQ, D_MODEL)),
        "w_in": n(ks[1], (L, D_MODEL, IN_WIDTH)) * D_MODEL ** -0.5,
        "sgu_ln_g": 1.0 + 0.02 * n(ks[2], (L, SGU_WIDTH)),
        "sgu_ln_b": 0.02 * n(ks[3], (L, SGU_WIDTH)),
        "sgu_w": n(ks[4], (L, SGU_GROUPS, CHUNK, CHUNK)) * CHUNK ** -0.5,
        "sgu_b": 1.0 + 0.02 * n(ks[5], (L, SGU_GROUPS, CHUNK)),
        "w_branch_a": n(ks[6], (L, D_MODEL, D_MODEL)) * (D_MODEL ** -0.5 * BETA),
        "w_branch_b": n(ks[7], (L, ATTN_WIDTH, D_MODEL)) * (ATTN_WIDTH ** -0.5 * BETA),
        "w_out": n(ks[8], (L, D_MODEL, D_MODEL)) * (D_MODEL ** -0.5 * BETA),
        "ln1_g": 1.0 + 0.02 * n(ks[9], (L, D_MODEL)),
        "ln1_b": 0.02 * n(ks[10], (L, D_MODEL)),
        "w_ffn_up": n(ks[11], (L, D_MODEL, FFN_DIM)) * D_MODEL ** -0.5,
        "w_ffn_down": n(ks[12], (L, FFN_DIM, D_MODEL)) * (FFN_DIM ** -0.5 * BETA),
        "ln2_g": 1.0 + 0.02 * n(ks[13], (L, D_MODEL)),
        "ln2_b": 0.02 * n(ks[14], (L, D_MODEL)),
    }


def reference(x, w_in, sgu_ln_g, sgu_ln_b, sgu_w, sgu_b, w_branch_a, w_branch_b, w_out,
              ln1_g, ln1_b, w_ffn_up, w_ffn_down, ln2_g, ln2_b):
    for i in range(DEPTH):
        x = hybrid_layer(x, w_in[i], sgu_ln_g[i], sgu_ln_b[i], sgu_w[i], sgu_b[i],
                         w_branch_a[i], w_branch_b[i], w_out[i], ln1_g[i], ln1_b[i],
                         w_ffn_up[i], w_ffn_down[i], ln2_g[i], ln2_b[i])
    return x
```

```python
from contextlib import ExitStack

import numpy as np
import concourse.bass as bass
import concourse.mybir as mybir
from concourse.bass_utils import run_bass_kernel_spmd

F32 = mybir.dt.float32
BF16 = mybir.dt.bfloat16
AF = mybir.ActivationFunctionType
ALU = mybir.AluOpType
AX = mybir.AxisListType

D = 1024
NG = 8
NH = 8
HD = 128
IH = 8
IDIM = 64
FFN = 4096
ALPHA = 2.0 ** 0.25
LN_EPS = 1e-5
C_U, C_VA, C_Q, C_K, C_V, C_QI, C_KI, C_WI, C_GA, C_GB = 0, 1024, 2048, 3072, 4096, 5120, 5632, 5696, 5704, 6728
INW = 7752
NEG = -30000.0
BIG = 1.0e30
NBIS = 15


class Res:
    __slots__ = ("name", "writer", "readers")

    def __init__(self, name=""):
        self.name = name
        self.writer = None
        self.readers = []


class Op:
    __slots__ = ("eng", "fn", "deps", "dma", "semkey", "sig", "sigval", "idx")

    def __init__(self, eng, fn, dma, semkey):
        self.eng = eng
        self.fn = fn
        self.deps = []
        self.dma = dma
        self.semkey = semkey
        self.sig = False
        self.sigval = 0


class Sched:
    def __init__(self, nc, plan=None):
        self.nc = nc
        self.plan = plan
        self.ops = []
        self.n = 0
        self.sems = {}
        self.engs = {"pe": nc.tensor, "act": nc.scalar, "dve": nc.vector,
                     "pool": nc.gpsimd, "sp": nc.sync}

    def add(self, eng, fn, reads=(), writes=(), dma=False, semkey=None, extra_deps=()):
        if self.plan is not None:
            return self._live(eng, fn, dma)
        op = Op(eng, fn, dma, semkey)
        op.idx = len(self.ops)
        deps = {}
        for r in reads:
            if r.writer is not None:
                deps[id(r.writer)] = (r.writer, "raw")
        for w in writes:
            if w.writer is not None:
                deps.setdefault(id(w.writer), (w.writer, "waw"))
            for rd in w.readers:
                deps.setdefault(id(rd), (rd, "war"))
        for d in extra_deps:
            deps[id(d)] = (d, "raw")
        for d, kind in deps.values():
            if d is op:
                continue
            if (not d.dma) and (not dma) and d.eng == eng:
                if eng == "pe":
                    continue
            op.deps.append(d)
        for w in writes:
            w.writer = op
            w.readers = []
        for r in reads:
            r.readers.append(op)
        self.ops.append(op)
        return op

    def barrier(self):
        if self.plan is not None:
            for e in ("pe", "act", "dve", "pool", "sp"):
                self._live(e, None, False)
            return
        tail = {}
        for op in self.ops:
            if op.fn is None:
                continue
            key = (op.eng, op.semkey) if op.dma else (op.eng, None)
            tail[key] = op
        deps = list(tail.values())
        for e in ("pe", "act", "dve", "pool", "sp"):
            o = Op(e, None, False, None)
            o.idx = len(self.ops)
            o.deps = [d for d in deps if not ((not d.dma) and d.eng == e)]
            self.ops.append(o)

    def make_plan(self):
        for op in self.ops:
            if op.dma:
                op.sig = True
            for d in op.deps:
                d.sig = True
        counts = {}
        for op in self.ops:
            if not op.sig:
                continue
            key = ("dma", op.semkey) if op.dma else ("eng", op.eng)
            counts[key] = counts.get(key, 0) + (16 if op.dma else 1)
            op.sigval = counts[key]
            op.semkey = key
        waited = {e: {} for e in self.engs}
        plan = []
        n_wait = 0
        for op in self.ops:
            w = waited[op.eng]
            need = {}
            for d in op.deps:
                k = d.semkey
                if need.get(k, 0) < d.sigval:
                    need[k] = d.sigval
            waits = []
            for k, v in need.items():
                if w.get(k, 0) >= v:
                    continue
                waits.append((k, v))
                w[k] = v
                n_wait += 1
            plan.append((op.eng, waits, op.semkey if op.sig else None))
        self.n_wait = n_wait
        return plan

    def _sem(self, key):
        if key not in self.sems:
            self.sems[key] = self.nc.alloc_semaphore(f"sm{len(self.sems)}")
        return self.sems[key]

    def _live(self, eng, fn, dma):
        peng, waits, sigkey = self.plan[self.n]
        assert peng == eng, (peng, eng, self.n)
        self.n += 1
        e = self.engs[eng]
        for k, v in waits:
            e.wait_ge(self._sem(k), v)
        if fn is None:
            return None
        ins = fn()
        if sigkey is not None:
            ins.then_inc(self._sem(sigkey), 16 if dma else 1)
        return None


def build(S, debug=False, phases="AB1B2C1C2"):
    nc0 = bass.Bass("TRN2", target_bir_lowering=False)
    s0 = Sched(nc0)
    program(nc0, s0, S, debug, phases)
    plan = s0.make_plan()
    nc = bass.Bass("TRN2", target_bir_lowering=False)
    s1 = Sched(nc, plan)
    program(nc, s1, S, debug, phases)
    assert s1.n == len(plan)
    s1.n_wait = s0.n_wait
    s1.n_ops = len(plan)
    return nc, s1


def program(nc, S_, S, debug, phases):
    NT = S // 128
    NB = S // 512
    TOPK = min(256, S // 4)
    ein = lambda n, s: nc.dram_tensor(n, s, F32, kind="ExternalInput").ap()
    skind = "ExternalOutput" if debug else "Internal"
    scr = lambda n, s, dt: nc.dram_tensor(n, s, dt, kind=skind).ap()

    x_d = ein("x", [S, D])
    xT_d = ein("xT", [D, S])
    win_d = ein("w_in", [D, INW])
    sgug_d = ein("sgu_ln_g", [1, D])
    sgub_d = ein("sgu_ln_b", [1, D])
    sguwT_d = ein("sgu_wT", [NG, 128, 128])
    sgubias_d = ein("sgu_b", [1, NG * 128])
    wba_d = ein("w_branch_a", [D, D])
    wbb_d = ein("w_branch_b", [D, D])
    wout_d = ein("w_out", [D, D])
    ln1g_d = ein("ln1_g", [1, D])
    ln1b_d = ein("ln1_b", [1, D])
    wup_d = ein("w_ffn_up", [D, FFN])
    wdn_d = ein("w_ffn_down", [FFN, D])
    ln2g_d = ein("ln2_g", [1, D])
    ln2b_d = ein("ln2_b", [1, D])
    out_d = nc.dram_tensor("out", [S, D], F32, kind="ExternalOutput").ap()

    maT_d = scr("maT", [D, S], BF16)
    qT_d = scr("qT", [D, S], BF16)
    kT_d = scr("kT", [D, S], BF16)
    V_d = scr("V", [S, D], BF16)
    qiT_d = scr("qiT", [IH * IDIM, S], BF16)
    kiT_d = scr("kiT", [IDIM, S], BF16)
    wi_d = scr("wi", [S, IH], F32)
    sgbT_d = scr("sgbT", [D, S], BF16)
    mask_d = scr("maskb", [S, S], BF16)
    bT_d = scr("bT", [D, S], BF16)
    x1_d = scr("x1", [S, D], F32)
    x1T_d = scr("x1T", [D, S], BF16)

    add = S_.add
    T, ACT, DVE, POOL, SP = nc.tensor, nc.scalar, nc.vector, nc.gpsimd, nc.sync

    with ExitStack() as es:
        def sb(name, shape, dt):
            return es.enter_context(nc.sbuf_tensor(name, shape, dt))

        PS = [es.enter_context(nc.psum_tensor(f"ps{i}", [128, 512], F32)) for i in range(8)]
        PSR = [Res(f"ps{i}") for i in range(8)]

        ident_f = sb("ident_f", [128, 128], F32)
        ident_b = sb("ident_b", [128, 128], BF16)
        cmask = sb("cmask", [128, 128], F32)
        ones_b = sb("ones_b", [128, 128], BF16)
        r_const = Res("const")
        add("pool", lambda: POOL.memset(ident_f[:], 1.0), writes=[r_const])
        add("pool", lambda: POOL.affine_select(out=ident_f[:], in_=ident_f[:], pattern=[[-1, 128]],
                                               compare_op=ALU.is_equal, fill=0.0, base=0, channel_multiplier=1),
            reads=[r_const], writes=[r_const])
        add("pool", lambda: POOL.tensor_copy(out=ident_b[:], in_=ident_f[:]), reads=[r_const], writes=[r_const])
        add("pool", lambda: POOL.memset(cmask[:], 0.0), writes=[r_const])
        add("pool", lambda: POOL.affine_select(out=cmask[:], in_=cmask[:], pattern=[[-1, 128]],
                                               compare_op=ALU.is_ge, fill=-BIG, base=0, channel_multiplier=1),
            reads=[r_const], writes=[r_const])
        add("pool", lambda: POOL.memset(ones_b[:], 1.0), writes=[r_const])
        negh = sb("negh", [128, 1], F32)
        add("pool", lambda: POOL.memset(negh[:], -0.5), writes=[r_const])

        def bcast_load(sb, name, src, n):
            t = sb(name, [128, n], F32)
            r = Res(name)
            add("sp", lambda: SP.dma_start(out=t[:], in_=src.partition_broadcast(128)), writes=[r], dma=True,
                semkey=name)
            return t, r

        def phase_A(es):
            def sb(name, shape, dt):
                return es.enter_context(nc.sbuf_tensor(name, shape, dt))
            xs = sb("xs", [128, 8, S], BF16)
            gu = sb("gu", [128, 8, S], BF16)
            wb = [sb(f"wb{i}", [128, 8, 512], BF16) for i in range(2)]
            wba = sb("wba", [128, 8, D], BF16)
            wsT = sb("wsT", [128, NG, 128], BF16)
            bs_row = sb("bs_row", [1, NG * 128], BF16)
            NST = 4
            stg = [sb(f"stg{i}", [128, 512], BF16) for i in range(NST)]
            stg_r = [Res(f"stg{i}") for i in range(NST)]
            vtm = [sb(f"vtm{i}", [128, D], F32) for i in range(2)]
            vln = [sb(f"vln{i}", [128, D], BF16) for i in range(2)]
            wba_f = wba[:].rearrange("p c n -> p (c n)").bitcast(F32)
            wba_h = wba[:].rearrange("p c n -> p (c n)")
            vtm = vtm + [wba_f[:, 0:1024], wba_f[:, 1024:2048]]
            vln = vln + [wba_h[:, 4096:5120], wba_h[:, 5120:6144]]
            vtm_r = [Res(f"vtm{i}") for i in range(4)]
            vln_r = [Res(f"vln{i}") for i in range(4)]
            stats = [sb(f"stats{i}", [128, 2, 6], F32) for i in range(4)]
            mv = [sb(f"mv{i}", [128, 2], F32) for i in range(4)]
            rstd = [sb(f"rstd{i}", [128, 1], F32) for i in range(4)]
            st_r = [Res(f"st{i}") for i in range(4)]
            sgf = [sb(f"sgf{i}", [128, 512], F32) for i in range(2)]
            sgf_r = [Res(f"sgf{i}") for i in range(2)]
            wi_t = [sb(f"wi_t{i}", [128, IH], F32) for i in range(2)]
            wi_r = [Res(f"wi_t{i}") for i in range(2)]
            g_bc, g_bc_r = bcast_load(sb, "sgug_bc", sgug_d, D)
            b_bc, b_bc_r = bcast_load(sb, "sgub_bc", sgub_d, D)

            xs_r = [Res(f"xs{tb}") for tb in range(NB)]
            gu_r = [[Res(f"gu{g}_{tb}") for tb in range(NT)] for g in range(8)]
            wb_r = [Res("wb0"), Res("wb1")]
            wba_r = Res("wba")
            ws_r = Res("wsT")

            def load_xs(tb):
                add("pool", (lambda tb=tb: POOL.dma_start(
                    out=xs[:, :, tb * 512:(tb + 1) * 512],
                    in_=xT_d[:, tb * 512:(tb + 1) * 512].rearrange("(c p) t -> p c t", p=128))),
                    writes=[xs_r[tb]], dma=True, semkey=f"xs{tb}")

            def load_sgu_consts():
                add("pool", lambda: POOL.dma_start(out=wsT[:], in_=sguwT_d.rearrange("g s t -> s g t")),
                    writes=[ws_r], dma=True, semkey="wsT")
                add("pool", lambda: POOL.dma_start(out=bs_row[:], in_=sgubias_d), writes=[ws_r], dma=True,
                    semkey="wsT")
                add("pool", lambda: POOL.affine_select(out=wsT[:], in_=wsT[:], pattern=[[0, NG], [1, 128]],
                                                       compare_op=ALU.is_ge, fill=0.0, base=0, channel_multiplier=-1),
                    reads=[ws_r], writes=[ws_r])

            def load_w(slot, c0, ncols):
                add("pool", lambda: POOL.dma_start(
                    out=wb[slot][:, :, :ncols],
                    in_=win_d[:, c0:c0 + ncols].rearrange("(c p) n -> p c n", p=128)),
                    writes=[wb_r[slot]], dma=True, semkey=f"wb{slot}")

            bank = [0]

            def next_bank(n=4, base=0):
                b = base + bank[0] % n
                bank[0] += 1
                return b

            stc = [0]

            def fm_block(slot, off, M, evac):
                for tb in range(NB):
                    b = next_bank()
                    for c in range(8):
                        add("pe", (lambda c=c, tb=tb, b=b: T.matmul(
                            PS[b][:M, :], wb[slot][:, c, off:off + M], xs[:, c, tb * 512:(tb + 1) * 512],
                            start=(c == 0), stop=(c == 7))),
                            reads=[wb_r[slot], xs_r[tb]], writes=[PSR[b]])
                    evac(tb, b)

            def store_evac(dst_rows, eng="act", func=None, M=128):
                def ev(tb, b):
                    i = stc[0] % NST
                    stc[0] += 1
                    if eng == "act":
                        add("act", lambda: ACT.activation(out=stg[i][:M, :], in_=PS[b][:M, :],
                                                          func=(func or AF.Copy)),
                            reads=[PSR[b]], writes=[stg_r[i]])
                    else:
                        add("dve", lambda: DVE.tensor_copy(out=stg[i][:M, :], in_=PS[b][:M, :]),
                            reads=[PSR[b]], writes=[stg_r[i]])
                    add("sp", lambda: SP.dma_start(out=dst_rows[:, tb * 512:(tb + 1) * 512], in_=stg[i][:M, :]),
                        reads=[stg_r[i]], dma=True, semkey=f"stg{i}")
                return ev

            blocks = [(C_U, 512), (C_U + 512, 512)]
            load_xs(0)
            load_w(0, *blocks[0])
            for tb in range(1, NB):
                load_xs(tb)
            load_w(1, *blocks[1])
            load_sgu_consts()
            for bi in range(2):
                for j in range(4):
                    g = bi * 4 + j

                    def ev(tb, b, g=g):
                        add("act", lambda: ACT.activation(out=gu[:, g, tb * 512:(tb + 1) * 512], in_=PS[b][:],
                                                          func=AF.Gelu_apprx_tanh),
                            reads=[PSR[b]], writes=[gu_r[g][tb * 4 + k] for k in range(4)])
                    fm_block(bi, j * 128, 128, ev)
            load_w(0, C_VA, 512)
            load_w(1, C_VA + 512, 512)
            pend_sgu = []

            def sgu_stage(tt, p, b0, b1):
                for hb, b in ((0, b0), (1, b1)):
                    for j in range(4):
                        g = hb * 4 + j
                        add("pe", (lambda g=g, j=j, b=b: T.matmul(
                            PS[b][:, j * 128:(j + 1) * 128], vln[p][:, g * 128:(g + 1) * 128], wsT[:, g, :],
                            start=True, stop=False)),
                            reads=[vln_r[p], ws_r], writes=[PSR[b]])
                        add("pe", (lambda g=g, j=j, b=b: T.matmul(
                            PS[b][:, j * 128:(j + 1) * 128], ones_b[0:1, :], bs_row[0:1, g * 128:(g + 1) * 128],
                            start=False, stop=True)),
                            reads=[ws_r, r_const], writes=[PSR[b]])
                    add("dve", (lambda hb=hb, b=b: DVE.tensor_tensor(
                        out=gu[:, hb * 4:(hb + 1) * 4, tt * 128:(tt + 1) * 128],
                        in0=gu[:, hb * 4:(hb + 1) * 4, tt * 128:(tt + 1) * 128],
                        in1=PS[b][:].rearrange("p (j t) -> p j t", j=4), op=ALU.mult)),
                        reads=[PSR[b]] + [gu_r[hb * 4 + j][tt] for j in range(4)],
                        writes=[gu_r[hb * 4 + j][tt] for j in range(4)])

            for tt in range(NT):
                p = tt % 4
                b0, b1 = 2 * p, 2 * p + 1
                for half, b in ((0, b0), (1, b1)):
                    for c in range(8):
                        add("pe", (lambda c=c, half=half, b=b: T.matmul(
                            PS[b][:], xs[:, c, tt * 128:(tt + 1) * 128], wb[half][:, c, :],
                            start=(c == 0), stop=(c == 7))),
                            reads=[wb_r[half], xs_r[tt // 4]], writes=[PSR[b]])
                    add("act", (lambda half=half, b=b: ACT.activation(
                        out=vtm[p][:, half * 512:(half + 1) * 512], in_=PS[b][:], func=AF.Gelu_apprx_tanh)),
                        reads=[PSR[b]], writes=[vtm_r[p]])
                for half in range(2):
                    add("dve", (lambda half=half: DVE.bn_stats(out=stats[p][:, half, :],
                                                               in_=vtm[p][:, half * 512:(half + 1) * 512])),
                        reads=[vtm_r[p]], writes=[st_r[p]])
                add("dve", lambda: DVE.bn_aggr(out=mv[p][:], in_=stats[p][:].rearrange("p a b -> p (a b)")),
                    reads=[st_r[p]], writes=[st_r[p]])
                add("dve", lambda: DVE.tensor_scalar(out=rstd[p][:], in0=mv[p][:, 1:2], scalar1=LN_EPS, scalar2=None,
                                                     op0=ALU.add),
                    reads=[st_r[p]], writes=[st_r[p]])
                add("pool", lambda: POOL.tensor_tensor(out=rstd[p][:], in0=rstd[p][:], in1=negh[:], op=ALU.pow),
                    reads=[st_r[p], r_const], writes=[st_r[p]])
                add("dve", lambda: DVE.scalar_tensor_tensor(out=vtm[p][:], in0=vtm[p][:], scalar=mv[p][:, 0:1],
                                                            in1=g_bc[:], op0=ALU.subtract, op1=ALU.mult),
                    reads=[st_r[p], vtm_r[p], g_bc_r], writes=[vtm_r[p]])
                add("act", lambda: ACT.activation(out=vtm[p][:], in_=vtm[p][:], func=AF.Copy, scale=rstd[p][:]),
                    reads=[vtm_r[p], st_r[p]], writes=[vtm_r[p]])
                add("pool", lambda: POOL.tensor_tensor(out=vln[p][:], in0=vtm[p][:], in1=b_bc[:], op=ALU.add),
                    reads=[vtm_r[p], b_bc_r], writes=[vln_r[p]])
                pend_sgu.append(lambda tt=tt, p=p, b0=b0, b1=b1: sgu_stage(tt, p, b0, b1))
                if len(pend_sgu) > 2:
                    pend_sgu.pop(0)()

            while pend_sgu:
                pend_sgu.pop(0)()
            add("pool", lambda: POOL.dma_start(out=wba[:], in_=wba_d.rearrange("(c p) n -> p c n", p=128)),
                writes=[wba_r, vtm_r[2], vtm_r[3], vln_r[2], vln_r[3]], dma=True, semkey="wba")
            load_w(0, C_GA, 512)
            load_w(1, C_GA + 512, 512)
            for bi in range(2):
                for j in range(4):
                    fb = bi * 4 + j

                    def ev(tb, b, fb=fb):
                        q = tb % 2
                        add("act", lambda: ACT.activation(out=sgf[q][:], in_=PS[b][:], func=AF.Sigmoid),
                            reads=[PSR[b]], writes=[sgf_r[q]])
                        b2 = next_bank()
                        for c in range(8):
                            add("pe", (lambda c=c: T.matmul(
                                PS[b2][:], wba[:, c, fb * 128:(fb + 1) * 128], gu[:, c, tb * 512:(tb + 1) * 512],
                                start=(c == 0), stop=(c == 7))),
                                reads=[wba_r] + [gu_r[c][tb * 4 + k] for k in range(4)], writes=[PSR[b2]])
                        i = stc[0] % NST
                        stc[0] += 1
                        add("dve", lambda: DVE.tensor_tensor(out=stg[i][:], in0=PS[b2][:], in1=sgf[q][:], op=ALU.mult),
                            reads=[PSR[b2], sgf_r[q]], writes=[stg_r[i]])
                        add("sp", lambda: SP.dma_start(out=maT_d[fb * 128:(fb + 1) * 128, tb * 512:(tb + 1) * 512],
                                                       in_=stg[i][:]),
                            reads=[stg_r[i]], dma=True, semkey=f"stg{i}")
                    fm_block(bi, j * 128, 128, ev)
            seq = [(C_Q, qT_d, None, "dve"), (C_K, kT_d, None, "dve"), (C_GB, sgbT_d, AF.Sigmoid, "act")]
            slot = 0
            for (c0, dst, func, eng) in seq:
                for bi in range(2):
                    load_w(slot, c0 + bi * 512, 512)
                    for j in range(4):
                        fb = bi * 4 + j
                        fm_block(slot, j * 128, 128, store_evac(dst[fb * 128:(fb + 1) * 128, :], eng, func))
                    slot ^= 1
            load_w(slot, C_QI, 512)
            for pr_ in range(IH // 2):
                fm_block(slot, pr_ * 128, 128, store_evac(qiT_d[pr_ * 128:(pr_ + 1) * 128, :], "dve", None))
            slot ^= 1
            load_w(slot, C_KI, 72)
            fm_block(slot, 0, 64, store_evac(kiT_d[:, :], "dve", None, M=64))
            wscale = float(IH ** -0.5 * IDIM ** -0.5)
            for tt in range(NT):
                b = next_bank()
                q = tt % 2
                for c in range(8):
                    add("pe", (lambda c=c, b=b: T.matmul(PS[b][:, :IH], xs[:, c, tt * 128:(tt + 1) * 128],
                                                         wb[slot][:, c, 64:64 + IH], start=(c == 0), stop=(c == 7))),
                        reads=[wb_r[slot], xs_r[tt // 4]], writes=[PSR[b]])
                add("dve", (lambda b=b, q=q: DVE.tensor_scalar(out=wi_t[q][:], in0=PS[b][:, :IH], scalar1=wscale,
                                                               scalar2=None, op0=ALU.mult)),
                    reads=[PSR[b]], writes=[wi_r[q]])
                add("sp", (lambda q=q: SP.dma_start(out=wi_d[tt * 128:(tt + 1) * 128, :], in_=wi_t[q][:])),
                    reads=[wi_r[q]], dma=True, semkey=f"wi{q}")
            slot ^= 1
            for bi in range(2):
                load_w(slot, C_V + bi * 512, 512)
                for tt in range(NT):
                    b = next_bank()
                    for c in range(8):
                        add("pe", (lambda c=c, b=b: T.matmul(PS[b][:], xs[:, c, tt * 128:(tt + 1) * 128],
                                                             wb[slot][:, c, :], start=(c == 0), stop=(c == 7))),
                            reads=[wb_r[slot], xs_r[tt // 4]], writes=[PSR[b]])
                    i = stc[0] % NST
                    stc[0] += 1
                    add("act", (lambda b=b, i=i: ACT.activation(out=stg[i][:], in_=PS[b][:], func=AF.Copy)),
                        reads=[PSR[b]], writes=[stg_r[i]])
                    add("sp", (lambda i=i, bi=bi: SP.dma_start(
                        out=V_d[tt * 128:(tt + 1) * 128, bi * 512:(bi + 1) * 512], in_=stg[i][:])),
                        reads=[stg_r[i]], dma=True, semkey=f"stg{i}")
                slot ^= 1

        def phase_B1(es):
            def sb(name, shape, dt):
                return es.enter_context(nc.sbuf_tensor(name, shape, dt))
            kiT = sb("kiT_sb", [128, S], BF16)
            kiT_r = Res("kiT")
            qi = [sb(f"qi{i}", [128, IH // 2, 512], BF16) for i in range(2)]
            qi_r = [Res(f"qi{i}") for i in range(2)]
            wg = [sb(f"wg{i}", [128, 4, IH], F32) for i in range(2)]
            wg_r = [Res(f"wg{i}") for i in range(2)]
            diag = [sb(f"diag{i}", [128, IH, 128], BF16) for i in range(2)]
            diag_r = [Res(f"diag{i}") for i in range(2)]
            NR = 6
            Rb = [sb(f"Rb{i}", [128, 512], BF16) for i in range(NR)]
            Rb_r = [Res(f"Rb{i}") for i in range(NR)]
            NI = 8
            I_sb = [sb(f"I_sb{i}", [128, S], F32) for i in range(NI)]
            I_r = [Res(f"I{i}") for i in range(NI)]
            mb = [sb(f"mb{i}", [128, S], BF16) for i in range(2)]
            mb_r = [Res(f"mb{i}") for i in range(2)]
            junkD = sb("junkD", [128, S], BF16)
            junkA = sb("junkA", [128, S], BF16)
            junkD_r, junkA_r = Res("junkD"), Res("junkA")
            negt = sb("negt", [128, 384], BF16)
            negt_r = Res("negt")
            cmask_b = sb("cmask_b", [128, 128], BF16)
            bis = [sb(f"bis{i}", [128, 8], F32) for i in range(4)]
            bis_r = [Res(f"bis{i}") for i in range(4)]
            hwt = [sb(f"hwt{i}", [128, NBIS + 1], F32) for i in range(4)]
            p2t = sb("p2t", [128, NBIS + 1], F32)
            thr_all = sb("thr_all", [128, 1], F32)
            add("pool", lambda: POOL.memset(negt[:], NEG), writes=[negt_r])
            add("pool", lambda: POOL.memset(thr_all[:], -1.0e29), writes=[negt_r])
            add("pool", lambda: POOL.tensor_copy(out=cmask_b[:], in_=cmask[:]), reads=[r_const], writes=[negt_r])
            for k in range(NBIS + 1):
                add("pool", (lambda k=k: POOL.memset(p2t[:, k:k + 1], 2.0 ** -(k + 1))), writes=[negt_r])
            add("sp", lambda: SP.dma_start(out=kiT[0:64, :], in_=kiT_d), writes=[kiT_r], dma=True, semkey="kiT")
            add("sp", lambda: SP.dma_start(out=kiT[64:128, :], in_=kiT_d), writes=[kiT_r], dma=True, semkey="kiT")
            cnt = {"bank": 0, "acc": 0, "R": 0}

            def scores(i, ib):
                grp, r4 = i // 4, i % 4
                gp = grp % 2
                p = i % 2
                S_i = (i + 1) * 128
                if r4 == 0:
                    add("sp", (lambda: SP.dma_start(
                        out=qi[gp][:], in_=qiT_d[:, grp * 512:(grp + 1) * 512].rearrange("(pr q) t -> q pr t", q=128))),
                        writes=[qi_r[gp]], dma=True, semkey=f"qi{gp}")
                    add("sp", (lambda: SP.dma_start(
                        out=wg[gp][:], in_=wi_d[grp * 512:(grp + 1) * 512, :].rearrange("(j p) h -> p j h", p=128))),
                        writes=[wg_r[gp]], dma=True, semkey=f"wg{gp}")
                add("pool", (lambda: POOL.tensor_tensor(
                    out=diag[p][:], in0=ident_f[:].unsqueeze(1).to_broadcast([128, IH, 128]),
                    in1=wg[gp][:, r4, :].unsqueeze(2).to_broadcast([128, IH, 128]), op=ALU.mult)),
                    reads=[wg_r[gp], r_const], writes=[diag_r[p]])
                nblk = (S_i + 511) // 512
                for sbk in range(nblk):
                    wd = min(512, S_i - sbk * 512)
                    ab = 4 + cnt["acc"] % 4
                    cnt["acc"] += 1
                    last = (sbk == nblk - 1)
                    for pr_ in range(IH // 2):
                        hs = (2 * pr_, 2 * pr_ + 1)
                        bs, ris = [], []
                        for h in hs:
                            b = cnt["bank"] % 4
                            cnt["bank"] += 1
                            ri = cnt["R"] % NR
                            cnt["R"] += 1
                            bs.append(b)
                            ris.append(ri)
                            r0 = (h % 2) * 64
                            add("pe", (lambda: T.matmul(PS[b][:, :wd],
                                                        qi[gp][r0:r0 + 64, h // 2, r4 * 128:(r4 + 1) * 128],
                                                        kiT[r0:r0 + 64, sbk * 512:sbk * 512 + wd],
                                                        start=True, stop=True)),
                                reads=[qi_r[gp], kiT_r], writes=[PSR[b]])
                        for h, b, ri in zip(hs, bs, ris):
                            if h in (0, 2, 4, 6):
                                add("act", (lambda: ACT.activation(out=Rb[ri][:, :wd], in_=PS[b][:, :wd], func=AF.Relu)),
                                    reads=[PSR[b]], writes=[Rb_r[ri]])
                            else:
                                add("dve", (lambda: DVE.tensor_scalar(out=Rb[ri][:, :wd], in0=PS[b][:, :wd],
                                                                      scalar1=0.0, scalar2=None, op0=ALU.max)),
                                    reads=[PSR[b]], writes=[Rb_r[ri]])
                        while len(pendq) > 0:
                            pendq.pop(0)()
                        for h, b, ri in zip(hs, bs, ris):
                            def dstage(h=h, ri=ri, ab=ab, wd=wd, last=last, p=p, sbk=sbk, ib=ib):
                                add("pe", (lambda: T.matmul(PS[ab][:, :wd], diag[p][:, h, :], Rb[ri][:, :wd],
                                                            start=(h == 0), stop=(h == IH - 1 and not last))),
                                    reads=[diag_r[p], Rb_r[ri]], writes=[PSR[ab]])
                                if h == IH - 1:
                                    if last:
                                        add("pe", (lambda: T.matmul(PS[ab][:, wd - 128:wd], ident_b[:], cmask_b[:],
                                                                    start=False, stop=True)),
                                            reads=[r_const, negt_r], writes=[PSR[ab]])
                                    add("act", (lambda: ACT.activation(out=I_sb[ib][:, sbk * 512:sbk * 512 + wd],
                                                                       in_=PS[ab][:, :wd], func=AF.Copy)),
                                        reads=[PSR[ab]], writes=[I_r[ib]])
                            pendq.append(dstage)

            pendq = []

            NQ = NT // 4

            def quad_scores(kq):
                for q in range(4):
                    scores(4 * kq + q, (4 * kq + q) % NI)
                while pendq:
                    pendq.pop(0)()

            qorder = list(range(NQ))[::-1]
            quad_scores(qorder[0])
            mbc = [0]
            for qpos, kq in enumerate(qorder):
                if qpos + 1 < NQ:
                    quad_scores(qorder[qpos + 1])
                tiles = [4 * kq + q for q in range(4)]
                ibs = [t % NI for t in tiles]
                Sis = [(t + 1) * 128 for t in tiles]
                need = [S_i > TOPK for S_i in Sis]
                on_act = [False, False, True, True]
                dch = [q for q in range(4) if need[q] and not on_act[q]]
                ach = [q for q in range(4) if need[q] and on_act[q]]
                for q in dch + ach:
                    i, ib, S_i, B = tiles[q], ibs[q], Sis[q], bis[q]
                    add("dve", (lambda: DVE.tensor_reduce(out=B[:, 0:1], in_=I_sb[ib][:, :S_i], axis=AX.X, op=ALU.max)),
                        reads=[I_r[ib]], writes=[bis_r[q]])
                    add("dve", (lambda: DVE.tensor_reduce(out=B[:, 1:2], in_=I_sb[ib][:, :TOPK], axis=AX.X,
                                                          op=ALU.min)),
                        reads=[I_r[ib]], writes=[bis_r[q]])
                for q in dch:
                    B = bis[q]
                    add("dve", (lambda: DVE.tensor_tensor(out=B[:, 2:3], in0=B[:, 0:1], in1=B[:, 1:2], op=ALU.subtract)),
                        reads=[bis_r[q]], writes=[bis_r[q]])
                for q in ach:
                    B = bis[q]
                    add("dve", (lambda: DVE.tensor_tensor(out=B[:, 2:3], in0=B[:, 1:2], in1=B[:, 0:1], op=ALU.subtract)),
                        reads=[bis_r[q]], writes=[bis_r[q]])
                    add("pool", (lambda: POOL.memset(B[:, 7:8], float(Sis[q] - 2 * TOPK + 1))), writes=[bis_r[q]])
                for q in dch + ach:
                    B = bis[q]
                    add("dve", (lambda: DVE.tensor_scalar(out=hwt[q][:], in0=p2t[:], scalar1=B[:, 2:3], scalar2=None,
                                                          op0=ALU.mult)),
                        reads=[bis_r[q], negt_r], writes=[bis_r[q]])
                for q in dch:
                    B = bis[q]
                    add("dve", (lambda: DVE.tensor_tensor(out=B[:, 3:4], in0=B[:, 1:2], in1=hwt[q][:, 0:1], op=ALU.add)),
                        reads=[bis_r[q]], writes=[bis_r[q]])
                for q in ach:
                    B = bis[q]
                    add("dve", (lambda: DVE.tensor_scalar(out=B[:, 3:4], in0=B[:, 1:2], scalar1=-1.0,
                                                          scalar2=hwt[q][:, 0:1], op0=ALU.mult, op1=ALU.add)),
                        reads=[bis_r[q]], writes=[bis_r[q]])
                for k in range(NBIS):
                    for q in dch:
                        ib, S_i, B = ibs[q], Sis[q], bis[q]
                        add("dve", (lambda: DVE.tensor_scalar(out=mb[q][:, :S_i], in0=I_sb[ib][:, :S_i],
                                                              scalar1=B[:, 3:4], scalar2=float(0.5 - TOPK),
                                                              op0=ALU.is_ge, op1=ALU.add, accum_out=B[:, 4:5])),
                            reads=[I_r[ib], bis_r[q]], writes=[bis_r[q], mb_r[q]])
                    for q in ach:
                        ib, S_i, B = ibs[q], Sis[q], bis[q]
                        jb, jb_r = (junkD, junkD_r) if q == 2 else (junkA, junkA_r)
                        add("act", (lambda: ACT.activation(out=jb[:, :S_i], in_=I_sb[ib][:, :S_i], func=AF.Sign,
                                                           bias=B[:, 3:4], scale=1.0, accum_out=B[:, 4:5])),
                            reads=[I_r[ib], bis_r[q]], writes=[bis_r[q], jb_r])
                    for q in dch:
                        B = bis[q]
                        add("dve", (lambda: DVE.tensor_scalar(out=B[:, 5:6], in0=B[:, 4:5], scalar1=0.5, scalar2=-0.5,
                                                              op0=ALU.min, op1=ALU.max)),
                            reads=[bis_r[q]], writes=[bis_r[q]])
                    for q in ach:
                        B = bis[q]
                        add("act", (lambda: ACT.activation(out=B[:, 5:6], in_=B[:, 4:5], func=AF.Sign,
                                                           bias=B[:, 7:8], scale=1.0)),
                            reads=[bis_r[q]], writes=[bis_r[q]])
                    for q in dch:
                        B = bis[q]
                        add("dve", (lambda: DVE.scalar_tensor_tensor(out=B[:, 3:4], in0=B[:, 5:6],
                                                                     scalar=hwt[q][:, k:k + 1], in1=B[:, 3:4],
                                                                     op0=ALU.mult, op1=ALU.add)),
                            reads=[bis_r[q]], writes=[bis_r[q]])
                    for q in ach:
                        B = bis[q]
                        add("act", (lambda: ACT.activation(out=B[:, 3:4], in_=B[:, 5:6], func=AF.Identity,
                                                           bias=B[:, 3:4], scale=hwt[q][:, k + 1:k + 2])),
                            reads=[bis_r[q]], writes=[bis_r[q]])
                for q in dch:
                    B = bis[q]
                    add("dve", (lambda: DVE.tensor_tensor(out=B[:, 6:7], in0=B[:, 3:4], in1=hwt[q][:, NBIS:NBIS + 1],
                                                          op=ALU.subtract)),
                        reads=[bis_r[q]], writes=[bis_r[q]])
                for q in ach:
                    B = bis[q]
                    add("dve", (lambda: DVE.tensor_scalar(out=B[:, 6:7], in0=B[:, 3:4], scalar1=-1.0,
                                                          scalar2=hwt[q][:, NBIS:NBIS + 1], op0=ALU.mult, op1=ALU.add)),
                        reads=[bis_r[q]], writes=[bis_r[q]])
                for q in range(4):
                    i, ib, S_i, B = tiles[q], ibs[q], Sis[q], bis[q]
                    r4 = i % 4
                    m = mbc[0] % 2
                    mbc[0] += 1
                    thr_ap = B[:, 6:7] if need[q] else thr_all[:, 0:1]
                    add("dve", (lambda: DVE.tensor_scalar(out=mb[m][:, :S_i], in0=I_sb[ib][:, :S_i],
                                                          scalar1=thr_ap, scalar2=NEG, op0=ALU.is_lt, op1=ALU.mult)),
                        reads=[I_r[ib], bis_r[q], negt_r], writes=[mb_r[m]])
                    add("sp", (lambda: SP.dma_start(out=mask_d[i * 128:(i + 1) * 128, :S_i], in_=mb[m][:, :S_i])),
                        reads=[mb_r[m]], dma=True, semkey=f"mb{m}")
                    if r4 < 3:
                        wn = (3 - r4) * 128
                        add("sp", (lambda: SP.dma_start(out=mask_d[i * 128:(i + 1) * 128, S_i:S_i + wn],
                                                        in_=negt[:, :wn])),
                            reads=[negt_r], dma=True, semkey="negt")

        def phase_B2(es):
            def sb(name, shape, dt):
                return es.enter_context(nc.sbuf_tensor(name, shape, dt))
            HP = 4
            kT = sb("kT_sb", [128, HP, S], BF16)
            kT_r = Res("kT")
            Vs = sb("V_sb", [128, NT, HP * 128], BF16)
            V_r = Res("V")
            mk = [sb(f"mk{i}", [128, 4, S], BF16) for i in range(2)]
            mk_r = [Res(f"mk{i}") for i in range(2)]
            qg = [sb(f"qg{i}", [128, HP, 512], BF16) for i in range(2)]
            qg_r = [Res(f"qg{i}") for i in range(2)]
            NP = 6
            Pb = [sb(f"Pb{i}", [128, 512], BF16) for i in range(NP)]
            Pb_r = [Res(f"Pb{i}") for i in range(NP)]
            rinv = [sb(f"rinv{i}", [128, 512], F32) for i in range(2)]
            rinv_r = [Res(f"rinv{i}") for i in range(2)]
            bo = [sb(f"bo{i}", [128, 512], BF16) for i in range(2)]
            bo_r = [Res(f"bo{i}") for i in range(2)]
            scale = float(HD ** -0.5)
            cnt = {"sc": 0, "P": 0, "acc": 0, "g": 0}
            pending = []

            def flush(keep=0):
                while len(pending) > keep:
                    pending.pop(0)()

            for hp in range(NH // HP):
                flush()
                for hh in range(HP):
                    add("sp", (lambda hh=hh: SP.dma_start(out=kT[:, hh, :],
                                                         in_=kT_d[(hp * HP + hh) * 128:(hp * HP + hh + 1) * 128, :])),
                        writes=[kT_r], dma=True, semkey="kT2")
                add("sp", lambda: SP.dma_start(
                    out=Vs[:], in_=V_d[:, hp * HP * 128:(hp + 1) * HP * 128].rearrange("(n p) f -> p n f", p=128)),
                    writes=[V_r], dma=True, semkey="V2")
                def grp_loads(hp, grp, gp):
                    Sg = (4 * grp + 4) * 128
                    add("sp", lambda: SP.dma_start(
                        out=mk[gp][:, :, :Sg],
                        in_=mask_d[grp * 512:(grp + 1) * 512, :Sg].rearrange("(j p) s -> p j s", p=128)),
                        writes=[mk_r[gp]], dma=True, semkey=f"mk{gp}")
                    add("sp", lambda: SP.dma_start(
                        out=qg[gp][:],
                        in_=qT_d[hp * HP * 128:(hp + 1) * HP * 128, grp * 512:(grp + 1) * 512].rearrange(
                            "(h d) t -> d h t", d=128)),
                        writes=[qg_r[gp]], dma=True, semkey=f"qg{gp}")

                for grp in range(NB):
                    gp = cnt["g"] % 2
                    cnt["g"] += 1
                    nsb = 4 * grp + 4
                    Sg = nsb * 128
                    if grp == 0:
                        grp_loads(hp, 0, gp)
                    if grp + 1 < NB:
                        grp_loads(hp, grp + 1, 1 - gp)
                    for hh in range(HP):
                        h = hp * HP + hh
                        ap_ = cnt["acc"] % 2
                        cnt["acc"] += 1
                        bo_b, rs_b = 4 + 2 * ap_, 5 + 2 * ap_
                        for sbk in range(nsb):
                            cb = cnt["sc"] % 4
                            cnt["sc"] += 1
                            pi = cnt["P"] % NP
                            cnt["P"] += 1
                            j0 = max(0, sbk - 4 * grp)
                            c0 = j0 * 128
                            add("pe", (lambda: T.matmul(PS[cb][:, c0:], kT[:, hh, sbk * 128:(sbk + 1) * 128],
                                                        qg[gp][:, hh, c0:], start=True, stop=False)),
                                reads=[kT_r, qg_r[gp]], writes=[PSR[cb]])
                            for j in range(j0, 4):
                                add("pe", (lambda: T.matmul(
                                    PS[cb][:, j * 128:(j + 1) * 128], mk[gp][:, j, sbk * 128:(sbk + 1) * 128],
                                    ident_b[:], start=False, stop=(j == 3))),
                                    reads=[mk_r[gp], r_const], writes=[PSR[cb]])
                            add("act", (lambda: ACT.activation(out=Pb[pi][:, c0:], in_=PS[cb][:, c0:], func=AF.Exp,
                                                               scale=scale)),
                                reads=[PSR[cb]], writes=[Pb_r[pi]])
                            flush(keep=1)

                            def stage2(hh=hh, h=h, ap_=ap_, bo_b=bo_b, rs_b=rs_b, sbk=sbk, nsb=nsb, pi=pi, grp=grp,
                                       c0=c0):
                                add("pe", (lambda: T.matmul(PS[bo_b][:, c0:], Vs[:, sbk, hh * 128:(hh + 1) * 128],
                                                            Pb[pi][:, c0:], start=(sbk == 0), stop=(sbk == nsb - 1))),
                                    reads=[V_r, Pb_r[pi]], writes=[PSR[bo_b]])
                                add("pe", (lambda: T.matmul(PS[rs_b][:, c0:], ones_b[:], Pb[pi][:, c0:],
                                                            start=(sbk == 0), stop=(sbk == nsb - 1))),
                                    reads=[r_const, Pb_r[pi]], writes=[PSR[rs_b]])
                                if sbk == nsb - 1:
                                    add("dve", lambda: DVE.reciprocal(out=rinv[ap_][:], in_=PS[rs_b][:]),
                                        reads=[PSR[rs_b]], writes=[rinv_r[ap_]])
                                    add("dve", lambda: DVE.tensor_tensor(out=bo[ap_][:], in0=PS[bo_b][:],
                                                                         in1=rinv[ap_][:], op=ALU.mult),
                                        reads=[PSR[bo_b], rinv_r[ap_]], writes=[bo_r[ap_]])
                                    add("pool", lambda: POOL.dma_start(
                                        out=bT_d[h * 128:(h + 1) * 128, grp * 512:(grp + 1) * 512], in_=bo[ap_][:]),
                                        reads=[bo_r[ap_]], dma=True, semkey=f"bo{ap_}")
                            pending.append(stage2)
            flush()

        def layernorm(t, t_r, dst, dst_r, gbc, gbc_r, bbc, bbc_r, stats, mv, rstd, st_r):
            for half in range(2):
                add("dve", (lambda half=half: DVE.bn_stats(out=stats[:, half, :], in_=t[:, half * 512:(half + 1) * 512])),
                    reads=[t_r], writes=[st_r])
            add("dve", lambda: DVE.bn_aggr(out=mv[:], in_=stats[:].rearrange("p a b -> p (a b)")),
                reads=[st_r], writes=[st_r])
            add("dve", lambda: DVE.tensor_scalar(out=rstd[:], in0=mv[:, 1:2], scalar1=LN_EPS, scalar2=None, op0=ALU.add),
                reads=[st_r], writes=[st_r])
            add("pool", lambda: POOL.tensor_tensor(out=rstd[:], in0=rstd[:], in1=negh[:], op=ALU.pow),
                reads=[st_r, r_const], writes=[st_r])
            add("dve", lambda: DVE.scalar_tensor_tensor(out=t[:], in0=t[:], scalar=mv[:, 0:1], in1=gbc[:],
                                                        op0=ALU.subtract, op1=ALU.mult),
                reads=[st_r, t_r, gbc_r], writes=[t_r])
            add("act", lambda: ACT.activation(out=t[:], in_=t[:], func=AF.Copy, scale=rstd[:]),
                reads=[t_r, st_r], writes=[t_r])
            return add("pool", lambda: POOL.tensor_tensor(out=dst[:], in0=t[:], in1=bbc[:], op=ALU.add),
                       reads=[t_r, bbc_r], writes=[dst_r])

        def phase_C1(es):
            def sb(name, shape, dt):
                return es.enter_context(nc.sbuf_tensor(name, shape, dt))
            wbb = sb("wbb", [128, 8, D], BF16)
            wout = sb("wout", [128, 8, D], BF16)
            w_r = Res("wC1")
            g1, g1_r = bcast_load(sb, "ln1g_bc", ln1g_d, D)
            b1, b1_r = bcast_load(sb, "ln1b_bc", ln1b_d, D)
            add("pool", lambda: POOL.dma_start(out=wbb[:], in_=wbb_d.rearrange("(c p) n -> p c n", p=128)),
                writes=[w_r], dma=True, semkey="wbb")
            add("pool", lambda: POOL.dma_start(out=wout[:], in_=wout_d.rearrange("(c p) n -> p c n", p=128)),
                writes=[w_r], dma=True, semkey="wout")
            bTg = [sb(f"bTg{i}", [128, 8, 512], BF16) for i in range(2)]
            sgb = [sb(f"sgb{i}", [128, 8, 512], BF16) for i in range(2)]
            mag = [sb(f"mag{i}", [128, 8, 512], BF16) for i in range(2)]
            in_r = [Res(f"inC1_{i}") for i in range(2)]
            mg = [sb(f"mg{i}", [128, 8, 512], BF16) for i in range(2)]
            mg_r = [Res(f"mg{i}") for i in range(2)]
            tmp = [sb(f"tmpc{i}", [128, 512], F32) for i in range(2)]
            tmp_r = [Res(f"tmpc{i}") for i in range(2)]
            NTB = 4
            xt = [sb(f"xt{i}", [128, D], F32) for i in range(NTB)]
            xt_r = [Res(f"xt{i}") for i in range(NTB)]
            rt = [sb(f"rt{i}", [128, D], F32) for i in range(NTB)]
            rt_r = [Res(f"rt{i}") for i in range(NTB)]
            x1t = [sb(f"x1t{i}", [128, D], F32) for i in range(NTB)]
            x1t_r = [Res(f"x1t{i}") for i in range(NTB)]
            x1Tg = [sb(f"x1Tg{i}", [128, 8, 512], BF16) for i in range(2)]
            x1Tg_r = [Res(f"x1Tg{i}") for i in range(2)]
            stats = [sb(f"c1stats{i}", [128, 2, 6], F32) for i in range(NTB)]
            mv = [sb(f"c1mv{i}", [128, 2], F32) for i in range(NTB)]
            rstd = [sb(f"c1rstd{i}", [128, 1], F32) for i in range(NTB)]
            st_r = [Res(f"c1st{i}") for i in range(NTB)]
            cnt = {"b": 0, "t": 0, "tile": 0}
            pend_tr = []
            for grp in range(NB):
                gp = grp % 2
                cs = slice(grp * 512, (grp + 1) * 512)
                for (dst, srcd, nm) in ((bTg, bT_d, "bTg"), (sgb, sgbT_d, "sgb"), (mag, maT_d, "mag")):
                    add("sp", (lambda dst=dst, srcd=srcd: SP.dma_start(
                        out=dst[gp][:], in_=srcd[:, cs].rearrange("(c p) t -> p c t", p=128))),
                        writes=[in_r[gp]], dma=True, semkey=f"{nm}{gp}")
                for fb in range(8):
                    b = cnt["b"] % 4
                    cnt["b"] += 1
                    ti = cnt["t"] % 2
                    cnt["t"] += 1
                    for c in range(8):
                        add("pe", (lambda c=c, b=b: T.matmul(PS[b][:], wbb[:, c, fb * 128:(fb + 1) * 128],
                                                             bTg[gp][:, c, :], start=(c == 0), stop=(c == 7))),
                            reads=[w_r, in_r[gp]], writes=[PSR[b]])
                    add("dve", (lambda b=b, ti=ti: DVE.tensor_tensor(out=tmp[ti][:], in0=PS[b][:], in1=sgb[gp][:, fb, :],
                                                                     op=ALU.mult)),
                        reads=[PSR[b], in_r[gp]], writes=[tmp_r[ti]])
                    add("pool", (lambda ti=ti: POOL.tensor_tensor(out=mg[gp][:, fb, :], in0=tmp[ti][:],
                                                                  in1=mag[gp][:, fb, :], op=ALU.add)),
                        reads=[tmp_r[ti], in_r[gp]], writes=[mg_r[gp]])
                for j in range(4):
                    tt = grp * 4 + j
                    p = cnt["tile"] % NTB
                    cnt["tile"] += 1
                    add("sp", (lambda p=p, tt=tt: SP.dma_start(out=xt[p][:], in_=x_d[tt * 128:(tt + 1) * 128, :])),
                        writes=[xt_r[p]], dma=True, semkey=f"xt{p}")
                    for half in range(2):
                        b = 4 + cnt["b"] % 2
                        cnt["b"] += 1
                        for c in range(8):
                            add("pe", (lambda c=c, b=b: T.matmul(PS[b][:], mg[gp][:, c, j * 128:(j + 1) * 128],
                                                                 wout[:, c, half * 512:(half + 1) * 512],
                                                                 start=(c == 0), stop=(c == 7))),
                                reads=[w_r, mg_r[gp]], writes=[PSR[b]])
                        add("dve", (lambda b=b, p=p: DVE.scalar_tensor_tensor(
                            out=rt[p][:, half * 512:(half + 1) * 512], in0=xt[p][:, half * 512:(half + 1) * 512],
                            scalar=ALPHA, in1=PS[b][:], op0=ALU.mult, op1=ALU.add)),
                            reads=[PSR[b], xt_r[p]], writes=[rt_r[p]])
                    layernorm(rt[p], rt_r[p], x1t[p], x1t_r[p], g1, g1_r, b1, b1_r, stats[p], mv[p], rstd[p], st_r[p])
                    add("pool", (lambda p=p, tt=tt: POOL.dma_start(out=x1_d[tt * 128:(tt + 1) * 128, :], in_=x1t[p][:])),
                        reads=[x1t_r[p]], dma=True, semkey=f"x1t{p}")
                    if len(pend_tr) >= 3:
                        pend_tr.pop(0)()

                    def tr_stage(p=p, j=j, gp=gp, cs=cs):
                        for q4 in range(2):
                            b = 6 + q4
                            for k in range(4):
                                c = q4 * 4 + k
                                add("pe", (lambda: T.transpose(PS[b][:, k * 128:(k + 1) * 128],
                                                               x1t[p][:, c * 128:(c + 1) * 128], ident_f[:])),
                                    reads=[x1t_r[p], r_const], writes=[PSR[b]])
                            add("act", (lambda: ACT.activation(
                                out=x1Tg[gp][:, q4 * 4:(q4 + 1) * 4, j * 128:(j + 1) * 128],
                                in_=PS[b][:].rearrange("p (k t) -> p k t", k=4), func=AF.Copy)),
                                reads=[PSR[b]], writes=[x1Tg_r[gp]])
                        if j == 3:
                            add("act", lambda: ACT.dma_start(out=x1T_d[:, cs].rearrange("(c p) t -> p c t", p=128),
                                                             in_=x1Tg[gp][:]),
                                reads=[x1Tg_r[gp]], dma=True, semkey=f"x1Tg{gp}")
                    pend_tr.append(tr_stage)
            while pend_tr:
                pend_tr.pop(0)()

        def phase_C2(es):
            def sb(name, shape, dt):
                return es.enter_context(nc.sbuf_tensor(name, shape, dt))
            TG = 256
            wup = sb("wup", [128, 8, FFN], BF16)
            wdn = sb("wdn", [128, FFN // 128, D], BF16)
            wup_r = [Res(f"wup{i}") for i in range(8)]
            wdn_r = [Res(f"wdn{i}") for i in range(8)]
            for c in range(8):
                add("pool", (lambda c=c: POOL.dma_start(
                    out=wup[:, :, c * 512:(c + 1) * 512],
                    in_=wup_d[:, c * 512:(c + 1) * 512].rearrange("(k p) n -> p k n", p=128))),
                    writes=[wup_r[c]], dma=True, semkey=f"wup{c}")
            for c in range(8):
                add("pool", (lambda c=c: POOL.dma_start(
                    out=wdn[:, c * 4:(c + 1) * 4, :],
                    in_=wdn_d[c * 512:(c + 1) * 512, :].rearrange("(c p) n -> p c n", p=128))),
                    writes=[wdn_r[c]], dma=True, semkey=f"wdn{c}")
            g2, g2_r = bcast_load(sb, "ln2g_bc", ln2g_d, D)
            b2, b2_r = bcast_load(sb, "ln2b_bc", ln2b_d, D)
            xTg = [sb(f"xTg{i}", [128, 8, TG], BF16) for i in range(2)]
            xTg_r = [Res(f"xTg{i}") for i in range(2)]
            hT = sb("hT", [128, FFN // 128, TG], BF16)
            hT_r = [Res(f"hT{i}") for i in range(FFN // 128)]
            ht = [sb(f"ht{i}", [128, TG], F32) for i in range(2)]
            ht_r = [Res(f"ht{i}") for i in range(2)]
            x1t = [sb(f"c2x1t{i}", [128, D], F32) for i in range(2)]
            x1t_r = [Res(f"c2x1t{i}") for i in range(2)]
            rt = [sb(f"c2rt{i}", [128, D], F32) for i in range(2)]
            rt_r = [Res(f"c2rt{i}") for i in range(2)]
            ot = [sb(f"c2ot{i}", [128, D], F32) for i in range(2)]
            ot_r = [Res(f"c2ot{i}") for i in range(2)]
            stats = [sb(f"c2stats{i}", [128, 2, 6], F32) for i in range(2)]
            mv = [sb(f"c2mv{i}", [128, 2], F32) for i in range(2)]
            rstd = [sb(f"c2rstd{i}", [128, 1], F32) for i in range(2)]
            st_r = [Res(f"c2st{i}") for i in range(2)]
            cnt = {"b": 0, "t": 0, "tile": 0}
            NFC = FFN // 128
            for grp in range(S // TG):
                gp = grp % 2
                cs = slice(grp * TG, (grp + 1) * TG)
                add("sp", lambda: SP.dma_start(out=xTg[gp][:], in_=x1T_d[:, cs].rearrange("(c p) t -> p c t", p=128)),
                    writes=[xTg_r[gp]], dma=True, semkey=f"xTg{gp}")
                for fc in range(NFC):
                    b = cnt["b"] % 4
                    cnt["b"] += 1
                    ti = cnt["t"] % 2
                    cnt["t"] += 1
                    for c in range(8):
                        add("pe", (lambda c=c, b=b: T.matmul(PS[b][:, :TG], wup[:, c, fc * 128:(fc + 1) * 128],
                                                             xTg[gp][:, c, :], start=(c == 0), stop=(c == 7))),
                            reads=[wup_r[fc // 4], xTg_r[gp]], writes=[PSR[b]])
                    add("act", (lambda b=b, ti=ti: ACT.activation(out=ht[ti][:], in_=PS[b][:, :TG], func=AF.Relu)),
                        reads=[PSR[b]], writes=[ht_r[ti]])
                    add("pool", (lambda ti=ti: POOL.tensor_tensor(out=hT[:, fc, :], in0=ht[ti][:], in1=ht[ti][:],
                                                                  op=ALU.mult)),
                        reads=[ht_r[ti]], writes=[hT_r[fc]])
                for j in range(TG // 128):
                    tt = grp * (TG // 128) + j
                    p = cnt["tile"] % 2
                    cnt["tile"] += 1
                    add("sp", (lambda p=p, tt=tt: SP.dma_start(out=x1t[p][:], in_=x1_d[tt * 128:(tt + 1) * 128, :])),
                        writes=[x1t_r[p]], dma=True, semkey=f"c2x1t{p}")
                    for half in range(2):
                        b = 4 + cnt["b"] % 4
                        cnt["b"] += 1
                        for fc in range(NFC):
                            add("pe", (lambda fc=fc, b=b: T.matmul(PS[b][:], hT[:, fc, j * 128:(j + 1) * 128],
                                                                   wdn[:, fc, half * 512:(half + 1) * 512],
                                                                   start=(fc == 0), stop=(fc == NFC - 1))),
                                reads=[wdn_r[fc // 4], hT_r[fc]], writes=[PSR[b]])
                        add("dve", (lambda b=b, p=p: DVE.scalar_tensor_tensor(
                            out=rt[p][:, half * 512:(half + 1) * 512], in0=x1t[p][:, half * 512:(half + 1) * 512],
                            scalar=ALPHA, in1=PS[b][:], op0=ALU.mult, op1=ALU.add)),
                            reads=[PSR[b], x1t_r[p]], writes=[rt_r[p]])
                    layernorm(rt[p], rt_r[p], ot[p], ot_r[p], g2, g2_r, b2, b2_r, stats[p], mv[p], rstd[p], st_r[p])
                    add("pool", (lambda p=p, tt=tt: POOL.dma_start(out=out_d[tt * 128:(tt + 1) * 128, :], in_=ot[p][:])),
                        reads=[ot_r[p]], dma=True, semkey=f"c2ot{p}")

        if "A" in phases:
            with ExitStack() as es_a:
                phase_A(es_a)
                S_.barrier()
        if "B1" in phases:
            with ExitStack() as es_b1:
                phase_B1(es_b1)
                S_.barrier()
        if "B2" in phases:
            with ExitStack() as es_b2:
                phase_B2(es_b2)
                S_.barrier()
        if "C1" in phases:
            with ExitStack() as es_c1:
                phase_C1(es_c1)
                S_.barrier()
        if "C2" in phases:
            with ExitStack() as es_c2:
                phase_C2(es_c2)
                S_.barrier()

        S_.barrier()


def make_in_maps(inputs, S, nb):
    f = lambda a: np.ascontiguousarray(np.asarray(a, dtype=np.float32))
    shared = {
        "w_in": f(inputs["w_in"][0]),
        "sgu_ln_g": f(inputs["sgu_ln_g"][0]).reshape(1, D),
        "sgu_ln_b": f(inputs["sgu_ln_b"][0]).reshape(1, D),
        "sgu_wT": f(np.transpose(np.asarray(inputs["sgu_w"][0]), (0, 2, 1))),
        "sgu_b": f(inputs["sgu_b"][0]).reshape(1, NG * 128),
        "w_branch_a": f(inputs["w_branch_a"][0]),
        "w_branch_b": f(inputs["w_branch_b"][0]),
        "w_out": f(inputs["w_out"][0]),
        "ln1_g": f(inputs["ln1_g"][0]).reshape(1, D),
        "ln1_b": f(inputs["ln1_b"][0]).reshape(1, D),
        "w_ffn_up": f(inputs["w_ffn_up"][0]),
        "w_ffn_down": f(inputs["w_ffn_down"][0]),
        "ln2_g": f(inputs["ln2_g"][0]).reshape(1, D),
        "ln2_b": f(inputs["ln2_b"][0]).reshape(1, D),
    }
    x = np.asarray(inputs["x"], dtype=np.float32)
    maps = []
    for b in range(nb):
        m = dict(shared)
        m["x"] = f(x[b, :S])
        m["xT"] = f(x[b, :S].T)
        maps.append(m)
    return maps


def kernel(**inputs):
    x = np.asarray(inputs["x"])
    B, S, _ = x.shape
    nc, _ = build(S)
    maps = make_in_maps(inputs, S, B)
    res = run_bass_kernel_spmd(nc, maps, core_ids=list(range(B)))
    return np.stack([res.results[b]["out"] for b in range(B)], axis=0).astype(np.float32)
```

```python
from contextlib import ExitStack

import numpy as np
import concourse.bass as bass
import concourse.mybir as mybir
from concourse.bass_utils import run_bass_kernel_spmd

F32 = mybir.dt.float32
BF16 = mybir.dt.bfloat16
AF = mybir.ActivationFunctionType
ALU = mybir.AluOpType
AX = mybir.AxisListType

D = 1024
NG = 8
NH = 8
HD = 128
IH = 8
IDIM = 64
FFN = 4096
ALPHA = 2.0 ** 0.25
LN_EPS = 1e-5
C_U, C_VA, C_Q, C_K, C_V, C_QI, C_KI, C_WI, C_GA, C_GB = 0, 1024, 2048, 3072, 4096, 5120, 5632, 5696, 5704, 6728
INW = 7752
NEG = -30000.0
BIG = 1.0e30
NBIS = 15


class Res:
    __slots__ = ("name", "writer", "readers")

    def __init__(self, name=""):
        self.name = name
        self.writer = None
        self.readers = []


class Op:
    __slots__ = ("eng", "fn", "deps", "dma", "semkey", "sig", "sigval", "idx")

    def __init__(self, eng, fn, dma, semkey):
        self.eng = eng
        self.fn = fn
        self.deps = []
        self.dma = dma
        self.semkey = semkey
        self.sig = False
        self.sigval = 0


class Sched:
    def __init__(self, nc, plan=None):
        self.nc = nc
        self.plan = plan
        self.ops = []
        self.n = 0
        self.sems = {}
        self.engs = {"pe": nc.tensor, "act": nc.scalar, "dve": nc.vector,
                     "pool": nc.gpsimd, "sp": nc.sync}

    def add(self, eng, fn, reads=(), writes=(), dma=False, semkey=None, extra_deps=()):
        if self.plan is not None:
            return self._live(eng, fn, dma)
        op = Op(eng, fn, dma, semkey)
        op.idx = len(self.ops)
        deps = {}
        for r in reads:
            if r.writer is not None:
                deps[id(r.writer)] = (r.writer, "raw")
        for w in writes:
            if w.writer is not None:
                deps.setdefault(id(w.writer), (w.writer, "waw"))
            for rd in w.readers:
                deps.setdefault(id(rd), (rd, "war"))
        for d in extra_deps:
            deps[id(d)] = (d, "raw")
        for d, kind in deps.values():
            if d is op:
                continue
            if (not d.dma) and (not dma) and d.eng == eng:
                if eng == "pe":
                    continue
            op.deps.append(d)
        for w in writes:
            w.writer = op
            w.readers = []
        for r in reads:
            r.readers.append(op)
        self.ops.append(op)
        return op

    def barrier(self):
        if self.plan is not None:
            for e in ("pe", "act", "dve", "pool", "sp"):
                self._live(e, None, False)
            return
        tail = {}
        for op in self.ops:
            if op.fn is None:
                continue
            key = (op.eng, op.semkey) if op.dma else (op.eng, None)
            tail[key] = op
        deps = list(tail.values())
        for e in ("pe", "act", "dve", "pool", "sp"):
            o = Op(e, None, False, None)
            o.idx = len(self.ops)
            o.deps = [d for d in deps if not ((not d.dma) and d.eng == e)]
            self.ops.append(o)

    def make_plan(self):
        for op in self.ops:
            if op.dma:
                op.sig = True
            for d in op.deps:
                d.sig = True
        counts = {}
        for op in self.ops:
            if not op.sig:
                continue
            key = ("dma", op.semkey) if op.dma else ("eng", op.eng)
            counts[key] = counts.get(key, 0) + (16 if op.dma else 1)
            op.sigval = counts[key]
            op.semkey = key
        waited = {e: {} for e in self.engs}
        plan = []
        n_wait = 0
        for op in self.ops:
            w = waited[op.eng]
            need = {}
            for d in op.deps:
                k = d.semkey
                if need.get(k, 0) < d.sigval:
                    need[k] = d.sigval
            waits = []
            for k, v in need.items():
                if w.get(k, 0) >= v:
                    continue
                waits.append((k, v))
                w[k] = v
                n_wait += 1
            plan.append((op.eng, waits, op.semkey if op.sig else None))
        self.n_wait = n_wait
        return plan

    def _sem(self, key):
        if key not in self.sems:
            self.sems[key] = self.nc.alloc_semaphore(f"sm{len(self.sems)}")
        return self.sems[key]

    def _live(self, eng, fn, dma):
        peng, waits, sigkey = self.plan[self.n]
        assert peng == eng, (peng, eng, self.n)
        self.n += 1
        e = self.engs[eng]
        for k, v in waits:
            e.wait_ge(self._sem(k), v)
        if fn is None:
            return None
        ins = fn()
        if sigkey is not None:
            ins.then_inc(self._sem(sigkey), 16 if dma else 1)
        return None


def build(S, debug=False, phases="AB1B2C1C2"):
    nc0 = bass.Bass("TRN2", target_bir_lowering=False)
    s0 = Sched(nc0)
    program(nc0, s0, S, debug, phases)
    plan = s0.make_plan()
    nc = bass.Bass("TRN2", target_bir_lowering=False)
    s1 = Sched(nc, plan)
    program(nc, s1, S, debug, phases)
    assert s1.n == len(plan)
    s1.n_wait = s0.n_wait
    s1.n_ops = len(plan)
    return nc, s1


def program(nc, S_, S, debug, phases):
    NT = S // 128
    NB = S // 512
    TOPK = min(256, S // 4)
    ein = lambda n, s: nc.dram_tensor(n, s, F32, kind="ExternalInput").ap()
    skind = "ExternalOutput" if debug else "Internal"
    scr = lambda n, s, dt: nc.dram_tensor(n, s, dt, kind=skind).ap()

    x_d = ein("x", [S, D])
    xT_d = ein("xT", [D, S])
    win_d = ein("w_in", [D, INW])
    sgug_d = ein("sgu_ln_g", [1, D])
    sgub_d = ein("sgu_ln_b", [1, D])
    sguwT_d = ein("sgu_wT", [NG, 128, 128])
    sgubias_d = ein("sgu_b", [1, NG * 128])
    wba_d = ein("w_branch_a", [D, D])
    wbb_d = ein("w_branch_b", [D, D])
    wout_d = ein("w_out", [D, D])
    ln1g_d = ein("ln1_g", [1, D])
    ln1b_d = ein("ln1_b", [1, D])
    wup_d = ein("w_ffn_up", [D, FFN])
    wdn_d = ein("w_ffn_down", [FFN, D])
    ln2g_d = ein("ln2_g", [1, D])
    ln2b_d = ein("ln2_b", [1, D])
    out_d = nc.dram_tensor("out", [S, D], F32, kind="ExternalOutput").ap()

    maT_d = scr("maT", [D, S], BF16)
    qT_d = scr("qT", [D, S], BF16)
    kT_d = scr("kT", [D, S], BF16)
    V_d = scr("V", [S, D], BF16)
    qiT_d = scr("qiT", [IH * IDIM, S], BF16)
    kiT_d = scr("kiT", [IDIM, S], BF16)
    wi_d = scr("wi", [S, IH], F32)
    sgbT_d = scr("sgbT", [D, S], BF16)
    mask_d = scr("maskb", [S, S], BF16)
    bT_d = scr("bT", [D, S], BF16)
    x1_d = scr("x1", [S, D], F32)
    x1T_d = scr("x1T", [D, S], BF16)

    add = S_.add
    T, ACT, DVE, POOL, SP = nc.tensor, nc.scalar, nc.vector, nc.gpsimd, nc.sync

    with ExitStack() as es:
        def sb(name, shape, dt):
            return es.enter_context(nc.sbuf_tensor(name, shape, dt))

        PS = [es.enter_context(nc.psum_tensor(f"ps{i}", [128, 512], F32)) for i in range(8)]
        PSR = [Res(f"ps{i}") for i in range(8)]

        ident_f = sb("ident_f", [128, 128], F32)
        ident_b = sb("ident_b", [128, 128], BF16)
        cmask = sb("cmask", [128, 128], F32)
        ones_b = sb("ones_b", [128, 128], BF16)
        r_const = Res("const")
        add("pool", lambda: POOL.memset(ident_f[:], 1.0), writes=[r_const])
        add("pool", lambda: POOL.affine_select(out=ident_f[:], in_=ident_f[:], pattern=[[-1, 128]],
                                               compare_op=ALU.is_equal, fill=0.0, base=0, channel_multiplier=1),
            reads=[r_const], writes=[r_const])
        add("pool", lambda: POOL.tensor_copy(out=ident_b[:], in_=ident_f[:]), reads=[r_const], writes=[r_const])
        add("pool", lambda: POOL.memset(cmask[:], 0.0), writes=[r_const])
        add("pool", lambda: POOL.affine_select(out=cmask[:], in_=cmask[:], pattern=[[-1, 128]],
                                               compare_op=ALU.is_ge, fill=-BIG, base=0, channel_multiplier=1),
            reads=[r_const], writes=[r_const])
        add("pool", lambda: POOL.memset(ones_b[:], 1.0), writes=[r_const])
        negh = sb("negh", [128, 1], F32)
        add("pool", lambda: POOL.memset(negh[:], -0.5), writes=[r_const])

        def bcast_load(sb, name, src, n):
            t = sb(name, [128, n], F32)
            r = Res(name)
            add("sp", lambda: SP.dma_start(out=t[:], in_=src.partition_broadcast(128)), writes=[r], dma=True,
                semkey=name)
            return t, r

        def phase_A(es):
            def sb(name, shape, dt):
                return es.enter_context(nc.sbuf_tensor(name, shape, dt))
            xs = sb("xs", [128, 8, S], BF16)
            gu = sb("gu", [128, 8, S], BF16)
            wb = [sb(f"wb{i}", [128, 8, 512], BF16) for i in range(2)]
            wba = sb("wba", [128, 8, D], BF16)
            wsT = sb("wsT", [128, NG, 128], BF16)
            bs_row = sb("bs_row", [1, NG * 128], BF16)
            NST = 4
            stg = [sb(f"stg{i}", [128, 512], BF16) for i in range(NST)]
            stg_r = [Res(f"stg{i}") for i in range(NST)]
            vtm = [sb(f"vtm{i}", [128, D], F32) for i in range(2)]
            vln = [sb(f"vln{i}", [128, D], BF16) for i in range(2)]
            wba_f = wba[:].rearrange("p c n -> p (c n)").bitcast(F32)
            wba_h = wba[:].rearrange("p c n -> p (c n)")
            vtm = vtm + [wba_f[:, 0:1024], wba_f[:, 1024:2048]]
            vln = vln + [wba_h[:, 4096:5120], wba_h[:, 5120:6144]]
            vtm_r = [Res(f"vtm{i}") for i in range(4)]
            vln_r = [Res(f"vln{i}") for i in range(4)]
            stats = [sb(f"stats{i}", [128, 2, 6], F32) for i in range(4)]
            mv = [sb(f"mv{i}", [128, 2], F32) for i in range(4)]
            rstd = [sb(f"rstd{i}", [128, 1], F32) for i in range(4)]
            st_r = [Res(f"st{i}") for i in range(4)]
            sgf = [sb(f"sgf{i}", [128, 512], F32) for i in range(2)]
            sgf_r = [Res(f"sgf{i}") for i in range(2)]
            wi_t = [sb(f"wi_t{i}", [128, IH], F32) for i in range(2)]
            wi_r = [Res(f"wi_t{i}") for i in range(2)]
            g_bc, g_bc_r = bcast_load(sb, "sgug_bc", sgug_d, D)
            b_bc, b_bc_r = bcast_load(sb, "sgub_bc", sgub_d, D)

            xs_r = [Res(f"xs{tb}") for tb in range(NB)]
            gu_r = [[Res(f"gu{g}_{tb}") for tb in range(NT)] for g in range(8)]
            wb_r = [Res("wb0"), Res("wb1")]
            wba_r = Res("wba")
            ws_r = Res("wsT")

            def load_xs(tb):
                add("pool", (lambda tb=tb: POOL.dma_start(
                    out=xs[:, :, tb * 512:(tb + 1) * 512],
                    in_=xT_d[:, tb * 512:(tb + 1) * 512].rearrange("(c p) t -> p c t", p=128))),
                    writes=[xs_r[tb]], dma=True, semkey=f"xs{tb}")

            def load_sgu_consts():
                add("pool", lambda: POOL.dma_start(out=wsT[:], in_=sguwT_d.rearrange("g s t -> s g t")),
                    writes=[ws_r], dma=True, semkey="wsT")
                add("pool", lambda: POOL.dma_start(out=bs_row[:], in_=sgubias_d), writes=[ws_r], dma=True,
                    semkey="wsT")
                add("pool", lambda: POOL.affine_select(out=wsT[:], in_=wsT[:], pattern=[[0, NG], [1, 128]],
                                                       compare_op=ALU.is_ge, fill=0.0, base=0, channel_multiplier=-1),
                    reads=[ws_r], writes=[ws_r])

            def load_w(slot, c0, ncols):
                add("pool", lambda: POOL.dma_start(
                    out=wb[slot][:, :, :ncols],
                    in_=win_d[:, c0:c0 + ncols].rearrange("(c p) n -> p c n", p=128)),
                    writes=[wb_r[slot]], dma=True, semkey=f"wb{slot}")

            bank = [0]

            def next_bank(n=4, base=0):
                b = base + bank[0] % n
                bank[0] += 1
                return b

            stc = [0]

            def fm_block(slot, off, M, evac):
                for tb in range(NB):
                    b = next_bank()
                    for c in range(8):
                        add("pe", (lambda c=c, tb=tb, b=b: T.matmul(
                            PS[b][:M, :], wb[slot][:, c, off:off + M], xs[:, c, tb * 512:(tb + 1) * 512],
                            start=(c == 0), stop=(c == 7))),
                            reads=[wb_r[slot], xs_r[tb]], writes=[PSR[b]])
                    evac(tb, b)

            def store_evac(dst_rows, eng="act", func=None, M=128):
                def ev(tb, b):
                    i = stc[0] % NST
                    stc[0] += 1
                    if eng == "act":
                        add("act", lambda: ACT.activation(out=stg[i][:M, :], in_=PS[b][:M, :],
                                                          func=(func or AF.Copy)),
                            reads=[PSR[b]], writes=[stg_r[i]])
                    else:
                        add("dve", lambda: DVE.tensor_copy(out=stg[i][:M, :], in_=PS[b][:M, :]),
                            reads=[PSR[b]], writes=[stg_r[i]])
                    add("sp", lambda: SP.dma_start(out=dst_rows[:, tb * 512:(tb + 1) * 512], in_=stg[i][:M, :]),
                        reads=[stg_r[i]], dma=True, semkey=f"stg{i}")
                return ev

            blocks = [(C_U, 512), (C_U + 512, 512)]
            load_xs(0)
            load_w(0, *blocks[0])
            for tb in range(1, NB):
                load_xs(tb)
            load_w(1, *blocks[1])
            load_sgu_consts()
            for bi in range(2):
                for j in range(4):
                    g = bi * 4 + j

                    def ev(tb, b, g=g):
                        add("act", lambda: ACT.activation(out=gu[:, g, tb * 512:(tb + 1) * 512], in_=PS[b][:],
                                                          func=AF.Gelu_apprx_tanh),
                            reads=[PSR[b]], writes=[gu_r[g][tb * 4 + k] for k in range(4)])
                    fm_block(bi, j * 128, 128, ev)
            load_w(0, C_VA, 512)
            load_w(1, C_VA + 512, 512)
            pend_sgu = []

            def sgu_stage(tt, p, b0, b1):
                for hb, b in ((0, b0), (1, b1)):
                    for j in range(4):
                        g = hb * 4 + j
                        add("pe", (lambda g=g, j=j, b=b: T.matmul(
                            PS[b][:, j * 128:(j + 1) * 128], vln[p][:, g * 128:(g + 1) * 128], wsT[:, g, :],
                            start=True, stop=False)),
                            reads=[vln_r[p], ws_r], writes=[PSR[b]])
                        add("pe", (lambda g=g, j=j, b=b: T.matmul(
                            PS[b][:, j * 128:(j + 1) * 128], ones_b[0:1, :], bs_row[0:1, g * 128:(g + 1) * 128],
                            start=False, stop=True)),
                            reads=[ws_r, r_const], writes=[PSR[b]])
                    add("dve", (lambda hb=hb, b=b: DVE.tensor_tensor(
                        out=gu[:, hb * 4:(hb + 1) * 4, tt * 128:(tt + 1) * 128],
                        in0=gu[:, hb * 4:(hb + 1) * 4, tt * 128:(tt + 1) * 128],
                        in1=PS[b][:].rearrange("p (j t) -> p j t", j=4), op=ALU.mult)),
                        reads=[PSR[b]] + [gu_r[hb * 4 + j][tt] for j in range(4)],
                        writes=[gu_r[hb * 4 + j][tt] for j in range(4)])

            for tt in range(NT):
                p = tt % 4
                b0, b1 = 2 * p, 2 * p + 1
                for half, b in ((0, b0), (1, b1)):
                    for c in range(8):
                        add("pe", (lambda c=c, half=half, b=b: T.matmul(
                            PS[b][:], xs[:, c, tt * 128:(tt + 1) * 128], wb[half][:, c, :],
                            start=(c == 0), stop=(c == 7))),
                            reads=[wb_r[half], xs_r[tt // 4]], writes=[PSR[b]])
                    add("act", (lambda half=half, b=b: ACT.activation(
                        out=vtm[p][:, half * 512:(half + 1) * 512], in_=PS[b][:], func=AF.Gelu_apprx_tanh)),
                        reads=[PSR[b]], writes=[vtm_r[p]])
                for half in range(2):
                    add("dve", (lambda half=half: DVE.bn_stats(out=stats[p][:, half, :],
                                                               in_=vtm[p][:, half * 512:(half + 1) * 512])),
                        reads=[vtm_r[p]], writes=[st_r[p]])
                add("dve", lambda: DVE.bn_aggr(out=mv[p][:], in_=stats[p][:].rearrange("p a b -> p (a b)")),
                    reads=[st_r[p]], writes=[st_r[p]])
                add("dve", lambda: DVE.tensor_scalar(out=rstd[p][:], in0=mv[p][:, 1:2], scalar1=LN_EPS, scalar2=None,
                                                     op0=ALU.add),
                    reads=[st_r[p]], writes=[st_r[p]])
                add("pool", lambda: POOL.tensor_tensor(out=rstd[p][:], in0=rstd[p][:], in1=negh[:], op=ALU.pow),
                    reads=[st_r[p], r_const], writes=[st_r[p]])
                add("dve", lambda: DVE.scalar_tensor_tensor(out=vtm[p][:], in0=vtm[p][:], scalar=mv[p][:, 0:1],
                                                            in1=g_bc[:], op0=ALU.subtract, op1=ALU.mult),
                    reads=[st_r[p], vtm_r[p], g_bc_r], writes=[vtm_r[p]])
                add("act", lambda: ACT.activation(out=vtm[p][:], in_=vtm[p][:], func=AF.Copy, scale=rstd[p][:]),
                    reads=[vtm_r[p], st_r[p]], writes=[vtm_r[p]])
                add("pool", lambda: POOL.tensor_tensor(out=vln[p][:], in0=vtm[p][:], in1=b_bc[:], op=ALU.add),
                    reads=[vtm_r[p], b_bc_r], writes=[vln_r[p]])
                pend_sgu.append(lambda tt=tt, p=p, b0=b0, b1=b1: sgu_stage(tt, p, b0, b1))
                if len(pend_sgu) > 2:
                    pend_sgu.pop(0)()

            while pend_sgu:
                pend_sgu.pop(0)()
            add("pool", lambda: POOL.dma_start(out=wba[:], in_=wba_d.rearrange("(c p) n -> p c n", p=128)),
                writes=[wba_r, vtm_r[2], vtm_r[3], vln_r[2], vln_r[3]], dma=True, semkey="wba")
            load_w(0, C_GA, 512)
            load_w(1, C_GA + 512, 512)
            for bi in range(2):
                for j in range(4):
                    fb = bi * 4 + j

                    def ev(tb, b, fb=fb):
                        q = tb % 2
                        add("act", lambda: ACT.activation(out=sgf[q][:], in_=PS[b][:], func=AF.Sigmoid),
                            reads=[PSR[b]], writes=[sgf_r[q]])
                        b2 = next_bank()
                        for c in range(8):
                            add("pe", (lambda c=c: T.matmul(
                                PS[b2][:], wba[:, c, fb * 128:(fb + 1) * 128], gu[:, c, tb * 512:(tb + 1) * 512],
                                start=(c == 0), stop=(c == 7))),
                                reads=[wba_r] + [gu_r[c][tb * 4 + k] for k in range(4)], writes=[PSR[b2]])
                        i = stc[0] % NST
                        stc[0] += 1
                        add("dve", lambda: DVE.tensor_tensor(out=stg[i][:], in0=PS[b2][:], in1=sgf[q][:], op=ALU.mult),
                            reads=[PSR[b2], sgf_r[q]], writes=[stg_r[i]])
                        add("sp", lambda: SP.dma_start(out=maT_d[fb * 128:(fb + 1) * 128, tb * 512:(tb + 1) * 512],
                                                       in_=stg[i][:]),
                            reads=[stg_r[i]], dma=True, semkey=f"stg{i}")
                    fm_block(bi, j * 128, 128, ev)
            seq = [(C_Q, qT_d, None, "dve"), (C_K, kT_d, None, "dve"), (C_GB, sgbT_d, AF.Sigmoid, "act")]
            slot = 0
            for (c0, dst, func, eng) in seq:
                for bi in range(2):
                    load_w(slot, c0 + bi * 512, 512)
                    for j in range(4):
                        fb = bi * 4 + j
                        fm_block(slot, j * 128, 128, store_evac(dst[fb * 128:(fb + 1) * 128, :], eng, func))
                    slot ^= 1
            load_w(slot, C_QI, 512)
            for pr_ in range(IH // 2):
                fm_block(slot, pr_ * 128, 128, store_evac(qiT_d[pr_ * 128:(pr_ + 1) * 128, :], "dve", None))
            slot ^= 1
            load_w(slot, C_KI, 72)
            fm_block(slot, 0, 64, store_evac(kiT_d[:, :], "dve", None, M=64))
            wscale = float(IH ** -0.5 * IDIM ** -0.5)
            for tt in range(NT):
                b = next_bank()
                q = tt % 2
                for c in range(8):
                    add("pe", (lambda c=c, b=b: T.matmul(PS[b][:, :IH], xs[:, c, tt * 128:(tt + 1) * 128],
                                                         wb[slot][:, c, 64:64 + IH], start=(c == 0), stop=(c == 7))),
                        reads=[wb_r[slot], xs_r[tt // 4]], writes=[PSR[b]])
                add("dve", (lambda b=b, q=q: DVE.tensor_scalar(out=wi_t[q][:], in0=PS[b][:, :IH], scalar1=wscale,
                                                               scalar2=None, op0=ALU.mult)),
                    reads=[PSR[b]], writes=[wi_r[q]])
                add("sp", (lambda q=q: SP.dma_start(out=wi_d[tt * 128:(tt + 1) * 128, :], in_=wi_t[q][:])),
                    reads=[wi_r[q]], dma=True, semkey=f"wi{q}")
            slot ^= 1
            for bi in range(2):
                load_w(slot, C_V + bi * 512, 512)
                for tt in range(NT):
                    b = next_bank()
                    for c in range(8):
                        add("pe", (lambda c=c, b=b: T.matmul(PS[b][:], xs[:, c, tt * 128:(tt + 1) * 128],
                                                             wb[slot][:, c, :], start=(c == 0), stop=(c == 7))),
                            reads=[wb_r[slot], xs_r[tt // 4]], writes=[PSR[b]])
                    i = stc[0] % NST
                    stc[0] += 1
                    add("act", (lambda b=b, i=i: ACT.activation(out=stg[i][:], in_=PS[b][:], func=AF.Copy)),
                        reads=[PSR[b]], writes=[stg_r[i]])
                    add("sp", (lambda i=i, bi=bi: SP.dma_start(
                        out=V_d[tt * 128:(tt + 1) * 128, bi * 512:(bi + 1) * 512], in_=stg[i][:])),
                        reads=[stg_r[i]], dma=True, semkey=f"stg{i}")
                slot ^= 1

        def phase_B1(es):
            def sb(name, shape, dt):
                return es.enter_context(nc.sbuf_tensor(name, shape, dt))
            kiT = sb("kiT_sb", [128, S], BF16)
            kiT_r = Res("kiT")
            qi = [sb(f"qi{i}", [128, IH // 2, 512], BF16) for i in range(2)]
            qi_r = [Res(f"qi{i}") for i in range(2)]
            wg = [sb(f"wg{i}", [128, 4, IH], F32) for i in range(2)]
            wg_r = [Res(f"wg{i}") for i in range(2)]
            diag = [sb(f"diag{i}", [128, IH, 128], BF16) for i in range(2)]
            diag_r = [Res(f"diag{i}") for i in range(2)]
            NR = 6
            Rb = [sb(f"Rb{i}", [128, 512], BF16) for i in range(NR)]
            Rb_r = [Res(f"Rb{i}") for i in range(NR)]
            NI = 8
            I_sb = [sb(f"I_sb{i}", [128, S], F32) for i in range(NI)]
            I_r = [Res(f"I{i}") for i in range(NI)]
            mb = [sb(f"mb{i}", [128, S], BF16) for i in range(2)]
            mb_r = [Res(f"mb{i}") for i in range(2)]
            junkD = sb("junkD", [128, S], BF16)
            junkA = sb("junkA", [128, S], BF16)
            junkD_r, junkA_r = Res("junkD"), Res("junkA")
            negt = sb("negt", [128, 384], BF16)
            negt_r = Res("negt")
            cmask_b = sb("cmask_b", [128, 128], BF16)
            bis = [sb(f"bis{i}", [128, 8], F32) for i in range(4)]
            bis_r = [Res(f"bis{i}") for i in range(4)]
            hwt = [sb(f"hwt{i}", [128, NBIS + 1], F32) for i in range(4)]
            p2t = sb("p2t", [128, NBIS + 1], F32)
            thr_all = sb("thr_all", [128, 1], F32)
            add("pool", lambda: POOL.memset(negt[:], NEG), writes=[negt_r])
            add("pool", lambda: POOL.memset(thr_all[:], -1.0e29), writes=[negt_r])
            add("pool", lambda: POOL.tensor_copy(out=cmask_b[:], in_=cmask[:]), reads=[r_const], writes=[negt_r])
            for k in range(NBIS + 1):
                add("pool", (lambda k=k: POOL.memset(p2t[:, k:k + 1], 2.0 ** -(k + 1))), writes=[negt_r])
            add("sp", lambda: SP.dma_start(out=kiT[0:64, :], in_=kiT_d), writes=[kiT_r], dma=True, semkey="kiT")
            add("sp", lambda: SP.dma_start(out=kiT[64:128, :], in_=kiT_d), writes=[kiT_r], dma=True, semkey="kiT")
            cnt = {"bank": 0, "acc": 0, "R": 0}

            def scores(i, ib):
                grp, r4 = i // 4, i % 4
                gp = grp % 2
                p = i % 2
                S_i = (i + 1) * 128
                if r4 == 0:
                    add("sp", (lambda: SP.dma_start(
                        out=qi[gp][:], in_=qiT_d[:, grp * 512:(grp + 1) * 512].rearrange("(pr q) t -> q pr t", q=128))),
                        writes=[qi_r[gp]], dma=True, semkey=f"qi{gp}")
                    add("sp", (lambda: SP.dma_start(
                        out=wg[gp][:], in_=wi_d[grp * 512:(grp + 1) * 512, :].rearrange("(j p) h -> p j h", p=128))),
                        writes=[wg_r[gp]], dma=True, semkey=f"wg{gp}")
                add("pool", (lambda: POOL.tensor_tensor(
                    out=diag[p][:], in0=ident_f[:].unsqueeze(1).to_broadcast([128, IH, 128]),
                    in1=wg[gp][:, r4, :].unsqueeze(2).to_broadcast([128, IH, 128]), op=ALU.mult)),
                    reads=[wg_r[gp], r_const], writes=[diag_r[p]])
                nblk = (S_i + 511) // 512
                for sbk in range(nblk):
                    wd = min(512, S_i - sbk * 512)
                    ab = 4 + cnt["acc"] % 4
                    cnt["acc"] += 1
                    last = (sbk == nblk - 1)
                    for pr_ in range(IH // 2):
                        hs = (2 * pr_, 2 * pr_ + 1)
                        bs, ris = [], []
                        for h in hs:
                            b = cnt["bank"] % 4
                            cnt["bank"] += 1
                            ri = cnt["R"] % NR
                            cnt["R"] += 1
                            bs.append(b)
                            ris.append(ri)
                            r0 = (h % 2) * 64
                            add("pe", (lambda: T.matmul(PS[b][:, :wd],
                                                        qi[gp][r0:r0 + 64, h // 2, r4 * 128:(r4 + 1) * 128],
                                                        kiT[r0:r0 + 64, sbk * 512:sbk * 512 + wd],
                                                        start=True, stop=True)),
                                reads=[qi_r[gp], kiT_r], writes=[PSR[b]])
                        for h, b, ri in zip(hs, bs, ris):
                            if h in (0, 2, 4, 6):
                                add("act", (lambda: ACT.activation(out=Rb[ri][:, :wd], in_=PS[b][:, :wd], func=AF.Relu)),
                                    reads=[PSR[b]], writes=[Rb_r[ri]])
                            else:
                                add("dve", (lambda: DVE.tensor_scalar(out=Rb[ri][:, :wd], in0=PS[b][:, :wd],
                                                                      scalar1=0.0, scalar2=None, op0=ALU.max)),
                                    reads=[PSR[b]], writes=[Rb_r[ri]])
                        while len(pendq) > 0:
                            pendq.pop(0)()
                        for h, b, ri in zip(hs, bs, ris):
                            def dstage(h=h, ri=ri, ab=ab, wd=wd, last=last, p=p, sbk=sbk, ib=ib):
                                add("pe", (lambda: T.matmul(PS[ab][:, :wd], diag[p][:, h, :], Rb[ri][:, :wd],
                                                            start=(h == 0), stop=(h == IH - 1 and not last))),
                                    reads=[diag_r[p], Rb_r[ri]], writes=[PSR[ab]])
                                if h == IH - 1:
                                    if last:
                                        add("pe", (lambda: T.matmul(PS[ab][:, wd - 128:wd], ident_b[:], cmask_b[:],
                                                                    start=False, stop=True)),
                                            reads=[r_const, negt_r], writes=[PSR[ab]])
                                    add("act", (lambda: ACT.activation(out=I_sb[ib][:, sbk * 512:sbk * 512 + wd],
                                                                       in_=PS[ab][:, :wd], func=AF.Copy)),
                                        reads=[PSR[ab]], writes=[I_r[ib]])
                            pendq.append(dstage)

            pendq = []

            NQ = NT // 4

            def quad_scores(kq):
                for q in range(4):
                    scores(4 * kq + q, (4 * kq + q) % NI)
                while pendq:
                    pendq.pop(0)()

            qorder = list(range(NQ))[::-1]
            quad_scores(qorder[0])
            mbc = [0]
            for qpos, kq in enumerate(qorder):
                if qpos + 1 < NQ:
                    quad_scores(qorder[qpos + 1])
                tiles = [4 * kq + q for q in range(4)]
                ibs = [t % NI for t in tiles]
                Sis = [(t + 1) * 128 for t in tiles]
                need = [S_i > TOPK for S_i in Sis]
                on_act = [False, False, True, True]
                dch = [q for q in range(4) if need[q] and not on_act[q]]
                ach = [q for q in range(4) if need[q] and on_act[q]]
                for q in dch + ach:
                    i, ib, S_i, B = tiles[q], ibs[q], Sis[q], bis[q]
                    add("dve", (lambda: DVE.tensor_reduce(out=B[:, 0:1], in_=I_sb[ib][:, :S_i], axis=AX.X, op=ALU.max)),
                        reads=[I_r[ib]], writes=[bis_r[q]])
                    add("dve", (lambda: DVE.tensor_reduce(out=B[:, 1:2], in_=I_sb[ib][:, :TOPK], axis=AX.X,
                                                          op=ALU.min)),
                        reads=[I_r[ib]], writes=[bis_r[q]])
                for q in dch:
                    B = bis[q]
                    add("dve", (lambda: DVE.tensor_tensor(out=B[:, 2:3], in0=B[:, 0:1], in1=B[:, 1:2], op=ALU.subtract)),
                        reads=[bis_r[q]], writes=[bis_r[q]])
                for q in ach:
                    B = bis[q]
                    add("dve", (lambda: DVE.tensor_tensor(out=B[:, 2:3], in0=B[:, 1:2], in1=B[:, 0:1], op=ALU.subtract)),
                        reads=[bis_r[q]], writes=[bis_r[q]])
                    add("pool", (lambda: POOL.memset(B[:, 7:8], float(Sis[q] - 2 * TOPK + 1))), writes=[bis_r[q]])
                for q in dch + ach:
                    B = bis[q]
                    add("dve", (lambda: DVE.tensor_scalar(out=hwt[q][:], in0=p2t[:], scalar1=B[:, 2:3], scalar2=None,
                                                          op0=ALU.mult)),
                        reads=[bis_r[q], negt_r], writes=[bis_r[q]])
                for q in dch:
                    B = bis[q]
                    add("dve", (lambda: DVE.tensor_tensor(out=B[:, 3:4], in0=B[:, 1:2], in1=hwt[q][:, 0:1], op=ALU.add)),
                        reads=[bis_r[q]], writes=[bis_r[q]])
                for q in ach:
                    B = bis[q]
                    add("dve", (lambda: DVE.tensor_scalar(out=B[:, 3:4], in0=B[:, 1:2], scalar1=-1.0,
                                                          scalar2=hwt[q][:, 0:1], op0=ALU.mult, op1=ALU.add)),
                        reads=[bis_r[q]], writes=[bis_r[q]])
                for k in range(NBIS):
                    for q in dch:
                        ib, S_i, B = ibs[q], Sis[q], bis[q]
                        add("dve", (lambda: DVE.tensor_scalar(out=mb[q][:, :S_i], in0=I_sb[ib][:, :S_i],
                                                              scalar1=B[:, 3:4], scalar2=float(0.5 - TOPK),
                                                              op0=ALU.is_ge, op1=ALU.add, accum_out=B[:, 4:5])),
                            reads=[I_r[ib], bis_r[q]], writes=[bis_r[q], mb_r[q]])
                    for q in ach:
                        ib, S_i, B = ibs[q], Sis[q], bis[q]
                        jb, jb_r = (junkD, junkD_r) if q == 2 else (junkA, junkA_r)
                        add("act", (lambda: ACT.activation(out=jb[:, :S_i], in_=I_sb[ib][:, :S_i], func=AF.Sign,
                                                           bias=B[:, 3:4], scale=1.0, accum_out=B[:, 4:5])),
                            reads=[I_r[ib], bis_r[q]], writes=[bis_r[q], jb_r])
                    for q in dch:
                        B = bis[q]
                        add("dve", (lambda: DVE.tensor_scalar(out=B[:, 5:6], in0=B[:, 4:5], scalar1=0.5, scalar2=-0.5,
                                                              op0=ALU.min, op1=ALU.max)),
                            reads=[bis_r[q]], writes=[bis_r[q]])
                    for q in ach:
                        B = bis[q]
                        add("act", (lambda: ACT.activation(out=B[:, 5:6], in_=B[:, 4:5], func=AF.Sign,
                                                           bias=B[:, 7:8], scale=1.0)),
                            reads=[bis_r[q]], writes=[bis_r[q]])
                    for q in dch:
                        B = bis[q]
                        add("dve", (lambda: DVE.scalar_tensor_tensor(out=B[:, 3:4], in0=B[:, 5:6],
                                                                     scalar=hwt[q][:, k:k + 1], in1=B[:, 3:4],
                                                                     op0=ALU.mult, op1=ALU.add)),
                            reads=[bis_r[q]], writes=[bis_r[q]])
                    for q in ach:
                        B = bis[q]
                        add("act", (lambda: ACT.activation(out=B[:, 3:4], in_=B[:, 5:6], func=AF.Identity,
                                                           bias=B[:, 3:4], scale=hwt[q][:, k + 1:k + 2])),
                            reads=[bis_r[q]], writes=[bis_r[q]])
                for q in dch:
                    B = bis[q]
                    add("dve", (lambda: DVE.tensor_tensor(out=B[:, 6:7], in0=B[:, 3:4], in1=hwt[q][:, NBIS:NBIS + 1],
                                                          op=ALU.subtract)),
                        reads=[bis_r[q]], writes=[bis_r[q]])
                for q in ach:
                    B = bis[q]
                    add("dve", (lambda: DVE.tensor_scalar(out=B[:, 6:7], in0=B[:, 3:4], scalar1=-1.0,
                                                          scalar2=hwt[q][:, NBIS:NBIS + 1], op0=ALU.mult, op1=ALU.add)),
                        reads=[bis_r[q]], writes=[bis_r[q]])
                for q in range(4):
                    i, ib, S_i, B = tiles[q], ibs[q], Sis[q], bis[q]
                    r4 = i % 4
                    m = mbc[0] % 2
                    mbc[0] += 1
                    thr_ap = B[:, 6:7] if need[q] else thr_all[:, 0:1]
                    add("dve", (lambda: DVE.tensor_scalar(out=mb[m][:, :S_i], in0=I_sb[ib][:, :S_i],
                                                          scalar1=thr_ap, scalar2=NEG, op0=ALU.is_lt, op1=ALU.mult)),
                        reads=[I_r[ib], bis_r[q], negt_r], writes=[mb_r[m]])
                    add("sp", (lambda: SP.dma_start(out=mask_d[i * 128:(i + 1) * 128, :S_i], in_=mb[m][:, :S_i])),
                        reads=[mb_r[m]], dma=True, semkey=f"mb{m}")
                    if r4 < 3:
                        wn = (3 - r4) * 128
                        add("sp", (lambda: SP.dma_start(out=mask_d[i * 128:(i + 1) * 128, S_i:S_i + wn],
                                                        in_=negt[:, :wn])),
                            reads=[negt_r], dma=True, semkey="negt")

        def phase_B2(es):
            def sb(name, shape, dt):
                return es.enter_context(nc.sbuf_tensor(name, shape, dt))
            HP = 4
            kT = sb("kT_sb", [128, HP, S], BF16)
            kT_r = Res("kT")
            Vs = sb("V_sb", [128, NT, HP * 128], BF16)
            V_r = Res("V")
            mk = [sb(f"mk{i}", [128, 4, S], BF16) for i in range(2)]
            mk_r = [Res(f"mk{i}") for i in range(2)]
            qg = [sb(f"qg{i}", [128, HP, 512], BF16) for i in range(2)]
            qg_r = [Res(f"qg{i}") for i in range(2)]
            NP = 6
            Pb = [sb(f"Pb{i}", [128, 512], BF16) for i in range(NP)]
            Pb_r = [Res(f"Pb{i}") for i in range(NP)]
            rinv = [sb(f"rinv{i}", [128, 512], F32) for i in range(2)]
            rinv_r = [Res(f"rinv{i}") for i in range(2)]
            bo = [sb(f"bo{i}", [128, 512], BF16) for i in range(2)]
            bo_r = [Res(f"bo{i}") for i in range(2)]
            scale = float(HD ** -0.5)
            cnt = {"sc": 0, "P": 0, "acc": 0, "g": 0}
            pending = []

            def flush(keep=0):
                while len(pending) > keep:
                    pending.pop(0)()

            for hp in range(NH // HP):
                flush()
                for hh in range(HP):
                    add("sp", (lambda hh=hh: SP.dma_start(out=kT[:, hh, :],
                                                         in_=kT_d[(hp * HP + hh) * 128:(hp * HP + hh + 1) * 128, :])),
                        writes=[kT_r], dma=True, semkey="kT2")
                add("sp", lambda: SP.dma_start(
                    out=Vs[:], in_=V_d[:, hp * HP * 128:(hp + 1) * HP * 128].rearrange("(n p) f -> p n f", p=128)),
                    writes=[V_r], dma=True, semkey="V2")
                def grp_loads(hp, grp, gp):
                    Sg = (4 * grp + 4) * 128
                    add("sp", lambda: SP.dma_start(
                        out=mk[gp][:, :, :Sg],
                        in_=mask_d[grp * 512:(grp + 1) * 512, :Sg].rearrange("(j p) s -> p j s", p=128)),
                        writes=[mk_r[gp]], dma=True, semkey=f"mk{gp}")
                    add("sp", lambda: SP.dma_start(
                        out=qg[gp][:],
                        in_=qT_d[hp * HP * 128:(hp + 1) * HP * 128, grp * 512:(grp + 1) * 512].rearrange(
                            "(h d) t -> d h t", d=128)),
                        writes=[qg_r[gp]], dma=True, semkey=f"qg{gp}")

                for grp in range(NB):
                    gp = cnt["g"] % 2
                    cnt["g"] += 1
                    nsb = 4 * grp + 4
                    Sg = nsb * 128
                    if grp == 0:
                        grp_loads(hp, 0, gp)
                    if grp + 1 < NB:
                        grp_loads(hp, grp + 1, 1 - gp)
                    for hh in range(HP):
                        h = hp * HP + hh
                        ap_ = cnt["acc"] % 2
                        cnt["acc"] += 1
                        bo_b, rs_b = 4 + 2 * ap_, 5 + 2 * ap_
                        for sbk in range(nsb):
                            cb = cnt["sc"] % 4
                            cnt["sc"] += 1
                            pi = cnt["P"] % NP
                            cnt["P"] += 1
                            j0 = max(0, sbk - 4 * grp)
                            c0 = j0 * 128
                            add("pe", (lambda: T.matmul(PS[cb][:, c0:], kT[:, hh, sbk * 128:(sbk + 1) * 128],
                                                        qg[gp][:, hh, c0:], start=True, stop=False)),
                                reads=[kT_r, qg_r[gp]], writes=[PSR[cb]])
                            for j in range(j0, 4):
                                add("pe", (lambda: T.matmul(
                                    PS[cb][:, j * 128:(j + 1) * 128], mk[gp][:, j, sbk * 128:(sbk + 1) * 128],
                                    ident_b[:], start=False, stop=(j == 3))),
                                    reads=[mk_r[gp], r_const], writes=[PSR[cb]])
                            add("act", (lambda: ACT.activation(out=Pb[pi][:, c0:], in_=PS[cb][:, c0:], func=AF.Exp,
                                                               scale=scale)),
                                reads=[PSR[cb]], writes=[Pb_r[pi]])
                            flush(keep=2)

                            def stage2(hh=hh, h=h, ap_=ap_, bo_b=bo_b, rs_b=rs_b, sbk=sbk, nsb=nsb, pi=pi, grp=grp,
                                       c0=c0):
                                add("pe", (lambda: T.matmul(PS[bo_b][:, c0:], Vs[:, sbk, hh * 128:(hh + 1) * 128],
                                                            Pb[pi][:, c0:], start=(sbk == 0), stop=(sbk == nsb - 1))),
                                    reads=[V_r, Pb_r[pi]], writes=[PSR[bo_b]])
                                add("pe", (lambda: T.matmul(PS[rs_b][:, c0:], ones_b[:], Pb[pi][:, c0:],
                                                            start=(sbk == 0), stop=(sbk == nsb - 1))),
                                    reads=[r_const, Pb_r[pi]], writes=[PSR[rs_b]])
                                if sbk == nsb - 1:
                                    add("dve", lambda: DVE.reciprocal(out=rinv[ap_][:], in_=PS[rs_b][:]),
                                        reads=[PSR[rs_b]], writes=[rinv_r[ap_]])
                                    add("dve", lambda: DVE.tensor_tensor(out=bo[ap_][:], in0=PS[bo_b][:],
                                                                         in1=rinv[ap_][:], op=ALU.mult),
                                        reads=[PSR[bo_b], rinv_r[ap_]], writes=[bo_r[ap_]])
                                    add("pool", lambda: POOL.dma_start(
                                        out=bT_d[h * 128:(h + 1) * 128, grp * 512:(grp + 1) * 512], in_=bo[ap_][:]),
                                        reads=[bo_r[ap_]], dma=True, semkey=f"bo{ap_}")
                            pending.append(stage2)
            flush()

        def layernorm(t, t_r, dst, dst_r, gbc, gbc_r, bbc, bbc_r, stats, mv, rstd, st_r):
            for half in range(2):
                add("dve", (lambda half=half: DVE.bn_stats(out=stats[:, half, :], in_=t[:, half * 512:(half + 1) * 512])),
                    reads=[t_r], writes=[st_r])
            add("dve", lambda: DVE.bn_aggr(out=mv[:], in_=stats[:].rearrange("p a b -> p (a b)")),
                reads=[st_r], writes=[st_r])
            add("dve", lambda: DVE.tensor_scalar(out=rstd[:], in0=mv[:, 1:2], scalar1=LN_EPS, scalar2=None, op0=ALU.add),
                reads=[st_r], writes=[st_r])
            add("pool", lambda: POOL.tensor_tensor(out=rstd[:], in0=rstd[:], in1=negh[:], op=ALU.pow),
                reads=[st_r, r_const], writes=[st_r])
            add("dve", lambda: DVE.scalar_tensor_tensor(out=t[:], in0=t[:], scalar=mv[:, 0:1], in1=gbc[:],
                                                        op0=ALU.subtract, op1=ALU.mult),
                reads=[st_r, t_r, gbc_r], writes=[t_r])
            add("act", lambda: ACT.activation(out=t[:], in_=t[:], func=AF.Copy, scale=rstd[:]),
                reads=[t_r, st_r], writes=[t_r])
            return add("pool", lambda: POOL.tensor_tensor(out=dst[:], in0=t[:], in1=bbc[:], op=ALU.add),
                       reads=[t_r, bbc_r], writes=[dst_r])

        def phase_C1(es):
            def sb(name, shape, dt):
                return es.enter_context(nc.sbuf_tensor(name, shape, dt))
            wbb = sb("wbb", [128, 8, D], BF16)
            wout = sb("wout", [128, 8, D], BF16)
            w_r = Res("wC1")
            g1, g1_r = bcast_load(sb, "ln1g_bc", ln1g_d, D)
            b1, b1_r = bcast_load(sb, "ln1b_bc", ln1b_d, D)
            add("pool", lambda: POOL.dma_start(out=wbb[:], in_=wbb_d.rearrange("(c p) n -> p c n", p=128)),
                writes=[w_r], dma=True, semkey="wbb")
            add("pool", lambda: POOL.dma_start(out=wout[:], in_=wout_d.rearrange("(c p) n -> p c n", p=128)),
                writes=[w_r], dma=True, semkey="wout")
            bTg = [sb(f"bTg{i}", [128, 8, 512], BF16) for i in range(2)]
            sgb = [sb(f"sgb{i}", [128, 8, 512], BF16) for i in range(2)]
            mag = [sb(f"mag{i}", [128, 8, 512], BF16) for i in range(2)]
            in_r = [Res(f"inC1_{i}") for i in range(2)]
            mg = [sb(f"mg{i}", [128, 8, 512], BF16) for i in range(2)]
            mg_r = [Res(f"mg{i}") for i in range(2)]
            tmp = [sb(f"tmpc{i}", [128, 512], F32) for i in range(2)]
            tmp_r = [Res(f"tmpc{i}") for i in range(2)]
            NTB = 4
            xt = [sb(f"xt{i}", [128, D], F32) for i in range(NTB)]
            xt_r = [Res(f"xt{i}") for i in range(NTB)]
            rt = [sb(f"rt{i}", [128, D], F32) for i in range(NTB)]
            rt_r = [Res(f"rt{i}") for i in range(NTB)]
            x1t = [sb(f"x1t{i}", [128, D], F32) for i in range(NTB)]
            x1t_r = [Res(f"x1t{i}") for i in range(NTB)]
            x1Tg = [sb(f"x1Tg{i}", [128, 8, 512], BF16) for i in range(2)]
            x1Tg_r = [Res(f"x1Tg{i}") for i in range(2)]
            stats = [sb(f"c1stats{i}", [128, 2, 6], F32) for i in range(NTB)]
            mv = [sb(f"c1mv{i}", [128, 2], F32) for i in range(NTB)]
            rstd = [sb(f"c1rstd{i}", [128, 1], F32) for i in range(NTB)]
            st_r = [Res(f"c1st{i}") for i in range(NTB)]
            cnt = {"b": 0, "t": 0, "tile": 0}
            pend_tr = []
            for grp in range(NB):
                gp = grp % 2
                cs = slice(grp * 512, (grp + 1) * 512)
                for (dst, srcd, nm) in ((bTg, bT_d, "bTg"), (sgb, sgbT_d, "sgb"), (mag, maT_d, "mag")):
                    add("sp", (lambda dst=dst, srcd=srcd: SP.dma_start(
                        out=dst[gp][:], in_=srcd[:, cs].rearrange("(c p) t -> p c t", p=128))),
                        writes=[in_r[gp]], dma=True, semkey=f"{nm}{gp}")
                for fb in range(8):
                    b = cnt["b"] % 4
                    cnt["b"] += 1
                    ti = cnt["t"] % 2
                    cnt["t"] += 1
                    for c in range(8):
                        add("pe", (lambda c=c, b=b: T.matmul(PS[b][:], wbb[:, c, fb * 128:(fb + 1) * 128],
                                                             bTg[gp][:, c, :], start=(c == 0), stop=(c == 7))),
                            reads=[w_r, in_r[gp]], writes=[PSR[b]])
                    add("dve", (lambda b=b, ti=ti: DVE.tensor_tensor(out=tmp[ti][:], in0=PS[b][:], in1=sgb[gp][:, fb, :],
                                                                     op=ALU.mult)),
                        reads=[PSR[b], in_r[gp]], writes=[tmp_r[ti]])
                    add("pool", (lambda ti=ti: POOL.tensor_tensor(out=mg[gp][:, fb, :], in0=tmp[ti][:],
                                                                  in1=mag[gp][:, fb, :], op=ALU.add)),
                        reads=[tmp_r[ti], in_r[gp]], writes=[mg_r[gp]])
                for j in range(4):
                    tt = grp * 4 + j
                    p = cnt["tile"] % NTB
                    cnt["tile"] += 1
                    add("sp", (lambda p=p, tt=tt: SP.dma_start(out=xt[p][:], in_=x_d[tt * 128:(tt + 1) * 128, :])),
                        writes=[xt_r[p]], dma=True, semkey=f"xt{p}")
                    for half in range(2):
                        b = 4 + cnt["b"] % 2
                        cnt["b"] += 1
                        for c in range(8):
                            add("pe", (lambda c=c, b=b: T.matmul(PS[b][:], mg[gp][:, c, j * 128:(j + 1) * 128],
                                                                 wout[:, c, half * 512:(half + 1) * 512],
                                                                 start=(c == 0), stop=(c == 7))),
                                reads=[w_r, mg_r[gp]], writes=[PSR[b]])
                        add("dve", (lambda b=b, p=p: DVE.scalar_tensor_tensor(
                            out=rt[p][:, half * 512:(half + 1) * 512], in0=xt[p][:, half * 512:(half + 1) * 512],
                            scalar=ALPHA, in1=PS[b][:], op0=ALU.mult, op1=ALU.add)),
                            reads=[PSR[b], xt_r[p]], writes=[rt_r[p]])
                    layernorm(rt[p], rt_r[p], x1t[p], x1t_r[p], g1, g1_r, b1, b1_r, stats[p], mv[p], rstd[p], st_r[p])
                    add("pool", (lambda p=p, tt=tt: POOL.dma_start(out=x1_d[tt * 128:(tt + 1) * 128, :], in_=x1t[p][:])),
                        reads=[x1t_r[p]], dma=True, semkey=f"x1t{p}")
                    if len(pend_tr) >= 3:
                        pend_tr.pop(0)()

                    def tr_stage(p=p, j=j, gp=gp, cs=cs):
                        for q4 in range(2):
                            b = 6 + q4
                            for k in range(4):
                                c = q4 * 4 + k
                                add("pe", (lambda: T.transpose(PS[b][:, k * 128:(k + 1) * 128],
                                                               x1t[p][:, c * 128:(c + 1) * 128], ident_f[:])),
                                    reads=[x1t_r[p], r_const], writes=[PSR[b]])
                            add("act", (lambda: ACT.activation(
                                out=x1Tg[gp][:, q4 * 4:(q4 + 1) * 4, j * 128:(j + 1) * 128],
                                in_=PS[b][:].rearrange("p (k t) -> p k t", k=4), func=AF.Copy)),
                                reads=[PSR[b]], writes=[x1Tg_r[gp]])
                        if j == 3:
                            add("act", lambda: ACT.dma_start(out=x1T_d[:, cs].rearrange("(c p) t -> p c t", p=128),
                                                             in_=x1Tg[gp][:]),
                                reads=[x1Tg_r[gp]], dma=True, semkey=f"x1Tg{gp}")
                    pend_tr.append(tr_stage)
            while pend_tr:
                pend_tr.pop(0)()

        def phase_C2(es):
            def sb(name, shape, dt):
                return es.enter_context(nc.sbuf_tensor(name, shape, dt))
            TG = 256
            wup = sb("wup", [128, 8, FFN], BF16)
            wdn = sb("wdn", [128, FFN // 128, D], BF16)
            wup_r = [Res(f"wup{i}") for i in range(8)]
            wdn_r = [Res(f"wdn{i}") for i in range(8)]
            for c in range(8):
                add("pool", (lambda c=c: POOL.dma_start(
                    out=wup[:, :, c * 512:(c + 1) * 512],
                    in_=wup_d[:, c * 512:(c + 1) * 512].rearrange("(k p) n -> p k n", p=128))),
                    writes=[wup_r[c]], dma=True, semkey=f"wup{c}")
            for c in range(8):
                add("pool", (lambda c=c: POOL.dma_start(
                    out=wdn[:, c * 4:(c + 1) * 4, :],
                    in_=wdn_d[c * 512:(c + 1) * 512, :].rearrange("(c p) n -> p c n", p=128))),
                    writes=[wdn_r[c]], dma=True, semkey=f"wdn{c}")
            g2, g2_r = bcast_load(sb, "ln2g_bc", ln2g_d, D)
            b2, b2_r = bcast_load(sb, "ln2b_bc", ln2b_d, D)
            xTg = [sb(f"xTg{i}", [128, 8, TG], BF16) for i in range(2)]
            xTg_r = [Res(f"xTg{i}") for i in range(2)]
            hT = sb("hT", [128, FFN // 128, TG], BF16)
            hT_r = [Res(f"hT{i}") for i in range(FFN // 128)]
            ht = [sb(f"ht{i}", [128, TG], F32) for i in range(2)]
            ht_r = [Res(f"ht{i}") for i in range(2)]
            x1t = [sb(f"c2x1t{i}", [128, D], F32) for i in range(2)]
            x1t_r = [Res(f"c2x1t{i}") for i in range(2)]
            rt = [sb(f"c2rt{i}", [128, D], F32) for i in range(2)]
            rt_r = [Res(f"c2rt{i}") for i in range(2)]
            ot = [sb(f"c2ot{i}", [128, D], F32) for i in range(2)]
            ot_r = [Res(f"c2ot{i}") for i in range(2)]
            stats = [sb(f"c2stats{i}", [128, 2, 6], F32) for i in range(2)]
            mv = [sb(f"c2mv{i}", [128, 2], F32) for i in range(2)]
            rstd = [sb(f"c2rstd{i}", [128, 1], F32) for i in range(2)]
            st_r = [Res(f"c2st{i}") for i in range(2)]
            cnt = {"b": 0, "t": 0, "tile": 0}
            NFC = FFN // 128
            for grp in range(S // TG):
                gp = grp % 2
                cs = slice(grp * TG, (grp + 1) * TG)
                add("sp", lambda: SP.dma_start(out=xTg[gp][:], in_=x1T_d[:, cs].rearrange("(c p) t -> p c t", p=128)),
                    writes=[xTg_r[gp]], dma=True, semkey=f"xTg{gp}")
                for fc in range(NFC):
                    b = cnt["b"] % 4
                    cnt["b"] += 1
                    ti = cnt["t"] % 2
                    cnt["t"] += 1
                    for c in range(8):
                        add("pe", (lambda c=c, b=b: T.matmul(PS[b][:, :TG], wup[:, c, fc * 128:(fc + 1) * 128],
                                                             xTg[gp][:, c, :], start=(c == 0), stop=(c == 7))),
                            reads=[wup_r[fc // 4], xTg_r[gp]], writes=[PSR[b]])
                    add("act", (lambda b=b, ti=ti: ACT.activation(out=ht[ti][:], in_=PS[b][:, :TG], func=AF.Relu)),
                        reads=[PSR[b]], writes=[ht_r[ti]])
                    add("pool", (lambda ti=ti: POOL.tensor_tensor(out=hT[:, fc, :], in0=ht[ti][:], in1=ht[ti][:],
                                                                  op=ALU.mult)),
                        reads=[ht_r[ti]], writes=[hT_r[fc]])
                for j in range(TG // 128):
                    tt = grp * (TG // 128) + j
                    p = cnt["tile"] % 2
                    cnt["tile"] += 1
                    add("sp", (lambda p=p, tt=tt: SP.dma_start(out=x1t[p][:], in_=x1_d[tt * 128:(tt + 1) * 128, :])),
                        writes=[x1t_r[p]], dma=True, semkey=f"c2x1t{p}")
                    for half in range(2):
                        b = 4 + cnt["b"] % 4
                        cnt["b"] += 1
                        for fc in range(NFC):
                            add("pe", (lambda fc=fc, b=b: T.matmul(PS[b][:], hT[:, fc, j * 128:(j + 1) * 128],
                                                                   wdn[:, fc, half * 512:(half + 1) * 512],
                                                                   start=(fc == 0), stop=(fc == NFC - 1))),
                                reads=[wdn_r[fc // 4], hT_r[fc]], writes=[PSR[b]])
                        add("dve", (lambda b=b, p=p: DVE.scalar_tensor_tensor(
                            out=rt[p][:, half * 512:(half + 1) * 512], in0=x1t[p][:, half * 512:(half + 1) * 512],
                            scalar=ALPHA, in1=PS[b][:], op0=ALU.mult, op1=ALU.add)),
                            reads=[PSR[b], x1t_r[p]], writes=[rt_r[p]])
                    layernorm(rt[p], rt_r[p], ot[p], ot_r[p], g2, g2_r, b2, b2_r, stats[p], mv[p], rstd[p], st_r[p])
                    add("pool", (lambda p=p, tt=tt: POOL.dma_start(out=out_d[tt * 128:(tt + 1) * 128, :], in_=ot[p][:])),
                        reads=[ot_r[p]], dma=True, semkey=f"c2ot{p}")

        if "A" in phases:
            with ExitStack() as es_a:
                phase_A(es_a)
                S_.barrier()
        if "B1" in phases:
            with ExitStack() as es_b1:
                phase_B1(es_b1)
                S_.barrier()
        if "B2" in phases:
            with ExitStack() as es_b2:
                phase_B2(es_b2)
                S_.barrier()
        if "C1" in phases:
            with ExitStack() as es_c1:
                phase_C1(es_c1)
                S_.barrier()
        if "C2" in phases:
            with ExitStack() as es_c2:
                phase_C2(es_c2)
                S_.barrier()

        S_.barrier()


def make_in_maps(inputs, S, nb):
    f = lambda a: np.ascontiguousarray(np.asarray(a, dtype=np.float32))
    shared = {
        "w_in": f(inputs["w_in"][0]),
        "sgu_ln_g": f(inputs["sgu_ln_g"][0]).reshape(1, D),
        "sgu_ln_b": f(inputs["sgu_ln_b"][0]).reshape(1, D),
        "sgu_wT": f(np.transpose(np.asarray(inputs["sgu_w"][0]), (0, 2, 1))),
        "sgu_b": f(inputs["sgu_b"][0]).reshape(1, NG * 128),
        "w_branch_a": f(inputs["w_branch_a"][0]),
        "w_branch_b": f(inputs["w_branch_b"][0]),
        "w_out": f(inputs["w_out"][0]),
        "ln1_g": f(inputs["ln1_g"][0]).reshape(1, D),
        "ln1_b": f(inputs["ln1_b"][0]).reshape(1, D),
        "w_ffn_up": f(inputs["w_ffn_up"][0]),
        "w_ffn_down": f(inputs["w_ffn_down"][0]),
        "ln2_g": f(inputs["ln2_g"][0]).reshape(1, D),
        "ln2_b": f(inputs["ln2_b"][0]).reshape(1, D),
    }
    x = np.asarray(inputs["x"], dtype=np.float32)
    maps = []
    for b in range(nb):
        m = dict(shared)
        m["x"] = f(x[b, :S])
        m["xT"] = f(x[b, :S].T)
        maps.append(m)
    return maps


def kernel(**inputs):
    x = np.asarray(inputs["x"])
    B, S, _ = x.shape
    nc, _ = build(S)
    maps = make_in_maps(inputs, S, B)
    res = run_bass_kernel_spmd(nc, maps, core_ids=list(range(B)))
    return np.stack([res.results[b]["out"] for b in range(B)], axis=0).astype(np.float32)
```
